# Optimizing a Trainium2 kernel written in Bass

```python
import jax, jax.numpy as jnp
from jax import lax
import numpy as np

D_MODEL = 1024
BATCH = 8
SEQ = 4096
DEPTH = 4

HEAD_DIM = 64
ROT_DIM = HEAD_DIM // 4
ROPE_THETA = 500000.0
N_MIXERS = 3
A_HEADS = D_MODEL // HEAD_DIM
A_KV_HEADS = 4
A_WINDOW = 128
WIN_BLK = 128
B_HEADS = D_MODEL // HEAD_DIM
B_GROUPS = ((128, 1), (512, 4), (2048, 16))
C_HEADS = D_MODEL // HEAD_DIM
MOBA_BLOCK = 256
MOBA_TOPK = 3
MOBA_QCHUNK = 16
D_FF = ((8 * D_MODEL + 3 * 256 - 1) // (3 * 256)) * 256
DN_ALPHA = (2 * DEPTH) ** 0.25
DN_BETA = (8 * DEPTH) ** -0.25
LN_EPS = 1e-5
NEG = -1e30

kernel_name = "hybrid_swa_dilated_moba_deepnorm"


def layer_norm(x, g, b):
    xf = x.astype(jnp.float32)
    mu = xf.mean(-1, keepdims=True)
    var = jnp.square(xf - mu).mean(-1, keepdims=True)
    return ((xf - mu) * lax.rsqrt(var + LN_EPS) * g.astype(jnp.float32) + b.astype(jnp.float32)).astype(x.dtype)


def rope_tables(positions):
    inv = ROPE_THETA ** (-jnp.arange(0, ROT_DIM, 2, dtype=jnp.float32) / ROT_DIM)
    ang = positions.astype(jnp.float32)[..., None] * inv
    return jnp.cos(ang), jnp.sin(ang)


def apply_rope(x, cos, sin):
    xf = x.astype(jnp.float32)
    half = ROT_DIM // 2
    x1, x2 = xf[..., :half], xf[..., half:ROT_DIM]
    c, s = cos[:, :, None, :], sin[:, :, None, :]
    return jnp.concatenate([x1 * c - x2 * s, x2 * c + x1 * s, xf[..., ROT_DIM:]], axis=-1).astype(x.dtype)


def block_attention(q, k, v, blk, n_back, with_prev, sink=None):
    B, L, Hq, dh = q.shape
    Hkv = k.shape[2]
    G = Hq // Hkv
    nb = L // blk
    qb = q.astype(jnp.float32).reshape(B, nb, blk, Hkv, G, dh)
    kb = k.astype(jnp.float32).reshape(B, nb, blk, Hkv, dh)
    vb = v.astype(jnp.float32).reshape(B, nb, blk, Hkv, dh)
    if with_prev:
        shift = lambda t: jnp.pad(t, ((0, 0), (1, 0), (0, 0), (0, 0), (0, 0)))[:, :-1]
        kb = jnp.concatenate([shift(kb), kb], axis=2)
        vb = jnp.concatenate([shift(vb), vb], axis=2)
    off = blk if with_prev else 0
    nk = kb.shape[2]
    s = jnp.einsum('bnqhgd,bnkhd->bnhgqk', qb, kb) * (dh ** -0.5)
    diff = jnp.arange(blk)[:, None] + off - jnp.arange(nk)[None, :]
    ok = ((diff >= 0) & (diff <= n_back))[None]
    if with_prev:
        ok = ok & ((jnp.arange(nb)[:, None, None] > 0) | (jnp.arange(nk)[None, None, :] >= blk))
    s = jnp.where(ok[None, :, None, None], s, NEG)
    m = s.max(-1)
    if sink is not None:
        sk = sink.astype(jnp.float32).reshape(1, 1, Hkv, G, 1)
        m = jnp.maximum(m, sk)
    p = jnp.exp(s - m[..., None])
    l = p.sum(-1)
    if sink is not None:
        l = l + jnp.exp(sk - m)
    o = jnp.einsum('bnhgqk,bnkhd->bnqhgd', p, vb) / jnp.transpose(l, (0, 1, 4, 2, 3))[..., None]
    lse = jnp.transpose(m + jnp.log(l), (0, 1, 4, 2, 3)).reshape(B, L, Hq)
    return o.reshape(B, L, Hq, dh), lse


def mixer_a(x, w_qkv, b_qkv, sinks, w_o, b_o, cos, sin):
    B, S, _ = x.shape
    nq, nkv = A_HEADS * HEAD_DIM, A_KV_HEADS * HEAD_DIM
    qkv = x @ w_qkv + b_qkv
    q = apply_rope(qkv[..., :nq].reshape(B, S, A_HEADS, HEAD_DIM), cos, sin)
    k = apply_rope(qkv[..., nq:nq + nkv].reshape(B, S, A_KV_HEADS, HEAD_DIM), cos, sin)
    v = qkv[..., nq + nkv:].reshape(B, S, A_KV_HEADS, HEAD_DIM)
    o, _ = block_attention(q, k, v, WIN_BLK, A_WINDOW - 1, True, sinks)
    return o.reshape(B, S, nq).astype(x.dtype) @ w_o + b_o


def dilated_attention(q, k, v, window, dil):
    B, S, H, dh = q.shape
    span = dil * WIN_BLK
    S_pad = -(-S // span) * span
    L = S_pad // dil

    def to_phase(t):
        t = jnp.pad(t, ((0, 0), (0, S_pad - S), (0, 0), (0, 0)))
        return jnp.transpose(t.reshape(B, L, dil, H, dh), (0, 2, 1, 3, 4)).reshape(B * dil, L, H, dh)

    o, lse = block_attention(to_phase(q), to_phase(k), to_phase(v), WIN_BLK, window // dil, True)
    o = jnp.transpose(o.reshape(B, dil, L, H, dh), (0, 2, 1, 3, 4)).reshape(B, S_pad, H, dh)[:, :S]
    lse = jnp.transpose(lse.reshape(B, dil, L, H), (0, 2, 1, 3)).reshape(B, S_pad, H)[:, :S]
    return o, lse


def mixer_b(x, w_qkv, w_o, cos, sin):
    B, S, _ = x.shape
    qkv = (x @ w_qkv).reshape(B, S, len(B_GROUPS), 3, B_HEADS, HEAD_DIM)
    outs, lses = [], []
    for g, (window, dil) in enumerate(B_GROUPS):
        q = apply_rope(qkv[:, :, g, 0], cos, sin)
        k = apply_rope(qkv[:, :, g, 1], cos, sin)
        o, lse = dilated_attention(q, k, qkv[:, :, g, 2], window, dil)
        outs.append(o)
        lses.append(lse)
    w = jax.nn.softmax(jnp.stack(lses), axis=0)
    o = jnp.sum(w[..., None] * jnp.stack(outs), axis=0)
    return o.reshape(B, S, B_HEADS * HEAD_DIM).astype(x.dtype) @ w_o


def mixer_c(x, w_qkv, w_o, cos, sin):
    B, S, _ = x.shape
    H, dh = C_HEADS, HEAD_DIM
    qkv = (x @ w_qkv).reshape(B, S, 3, H, dh)
    q = apply_rope(qkv[:, :, 0], cos, sin)
    k = apply_rope(qkv[:, :, 1], cos, sin)
    v = qkv[:, :, 2]
    S_pad = -(-S // MOBA_BLOCK) * MOBA_BLOCK
    nblk = S_pad // MOBA_BLOCK
    pad = ((0, 0), (0, S_pad - S), (0, 0), (0, 0))
    q, k, v = jnp.pad(q, pad), jnp.pad(k, pad), jnp.pad(v, pad)
    o_self, lse_self = block_attention(q, k, v, MOBA_BLOCK, MOBA_BLOCK - 1, False)
    qf = q.astype(jnp.float32)
    kblk = k.reshape(B, nblk, MOBA_BLOCK, H, dh)
    vblk = v.reshape(B, nblk, MOBA_BLOCK, H, dh)
    kmean = kblk.astype(jnp.float32).mean(axis=2)
    gate = jnp.einsum('bshd,bnhd->bshn', qf, kmean)
    n_past = jnp.arange(S_pad) // MOBA_BLOCK
    gate = jnp.where((jnp.arange(nblk)[None, :] < n_past[:, None])[None, :, None, :], gate, NEG)
    topk = min(MOBA_TOPK, nblk)
    _, idx = lax.top_k(gate, topk)
    valid = jnp.arange(topk)[None, :] < n_past[:, None]
    kbt = jnp.transpose(kblk, (0, 3, 1, 2, 4))
    vbt = jnp.transpose(vblk, (0, 3, 1, 2, 4))
    nc = S_pad // MOBA_QCHUNK
    q_c = jnp.transpose(qf.reshape(B, nc, MOBA_QCHUNK, H, dh), (1, 0, 2, 3, 4))
    i_c = jnp.transpose(idx.reshape(B, nc, MOBA_QCHUNK, H, topk), (1, 0, 2, 3, 4))
    v_c = valid.reshape(nc, MOBA_QCHUNK, topk)
    bi = jnp.arange(B)[:, None, None, None]
    hi = jnp.arange(H)[None, None, :, None]
    scale = dh ** -0.5

    def chunk(args):
        qc, ic, vc = args
        kg = kbt[bi, hi, ic].astype(jnp.float32)
        vg = vbt[bi, hi, ic].astype(jnp.float32)
        s = jnp.einsum('bqhd,bqhjkd->bqhjk', qc, kg) * scale
        s = jnp.where(vc[None, :, None, :, None], s, NEG)
        m = s.max(axis=(-2, -1))
        p = jnp.exp(s - m[..., None, None])
        l = p.sum(axis=(-2, -1))
        o = jnp.einsum('bqhjk,bqhjkd->bqhd', p, vg) / l[..., None]
        return o, m + jnp.log(l)

    o_sel, lse_sel = lax.map(chunk, (q_c, i_c, v_c))
    o_sel = jnp.transpose(o_sel, (1, 0, 2, 3, 4)).reshape(B, S_pad, H, dh)
    lse_sel = jnp.transpose(lse_sel, (1, 0, 2, 3)).reshape(B, S_pad, H)
    m = jnp.maximum(lse_self, lse_sel)
    w1 = jnp.exp(lse_self - m)
    w2 = jnp.exp(lse_sel - m)
    o = (w1[..., None] * o_self + w2[..., None] * o_sel) / (w1 + w2)[..., None]
    return o[:, :S].reshape(B, S, H * dh).astype(x.dtype) @ w_o


def swiglu(x, w_gate_up, w_down):
    gu = x @ w_gate_up
    gate, up = gu[..., :D_FF], gu[..., D_FF:]
    return (jax.nn.silu(gate) * up) @ w_down


def setup_inputs(seed: int = 0) -> dict:
    key = jax.random.key(seed)
    ks = jax.random.split(key, 16)
    n_a = (DEPTH + 2) // 3
    n_b = (DEPTH + 1) // 3
    n_c = DEPTH // 3
    D = D_MODEL
    a_cols = (A_HEADS + 2 * A_KV_HEADS) * HEAD_DIM
    b_cols = len(B_GROUPS) * 3 * B_HEADS * HEAD_DIM
    c_cols = 3 * C_HEADS * HEAD_DIM
    nrm = lambda k, shape, fan_in, s=1.0: jax.random.normal(k, shape, jnp.float32) * (fan_in ** -0.5) * s
    return {
        "x": jax.random.normal(ks[0], (BATCH, SEQ, D), jnp.float32),
        "positions": jnp.broadcast_to(jnp.arange(SEQ, dtype=jnp.int32), (BATCH, SEQ)),
        "ln_g": 1.0 + 0.02 * jax.random.normal(ks[1], (DEPTH, 2, D), jnp.float32),
        "ln_b": 0.02 * jax.random.normal(ks[2], (DEPTH, 2, D), jnp.float32),
        "a_w_qkv": nrm(ks[3], (n_a, D, a_cols), D),
        "a_b_qkv": 0.02 * jax.random.normal(ks[4], (n_a, a_cols), jnp.float32),
        "a_sinks": 0.5 * jax.random.normal(ks[5], (n_a, A_HEADS), jnp.float32),
        "a_w_o": nrm(ks[6], (n_a, A_HEADS * HEAD_DIM, D), A_HEADS * HEAD_DIM, DN_BETA),
        "a_b_o": 0.02 * jax.random.normal(ks[7], (n_a, D), jnp.float32),
        "b_w_qkv": nrm(ks[8], (n_b, D, b_cols), D),
        "b_w_o": nrm(ks[9], (n_b, B_HEADS * HEAD_DIM, D), B_HEADS * HEAD_DIM, DN_BETA),
        "c_w_qkv": nrm(ks[10], (n_c, D, c_cols), D),
        "c_w_o": nrm(ks[11], (n_c, C_HEADS * HEAD_DIM, D), C_HEADS * HEAD_DIM, DN_BETA),
        "w_gate_up": nrm(ks[12], (DEPTH, D, 2 * D_FF), D),
        "w_down": nrm(ks[13], (DEPTH, D_FF, D), D_FF, DN_BETA),
    }


def reference(x, positions, ln_g, ln_b, a_w_qkv, a_b_qkv, a_sinks, a_w_o, a_b_o,
              b_w_qkv, b_w_o, c_w_qkv, c_w_o, w_gate_up, w_down):
    cos, sin = rope_tables(positions)
    for i in range(DEPTH):
        kind, j = i % N_MIXERS, i // N_MIXERS
        if kind == 0:
            h = mixer_a(x, a_w_qkv[j], a_b_qkv[j], a_sinks[j], a_w_o[j], a_b_o[j], cos, sin)
        elif kind == 1:
            h = mixer_b(x, b_w_qkv[j], b_w_o[j], cos, sin)
        else:
            h = mixer_c(x, c_w_qkv[j], c_w_o[j], cos, sin)
        x = layer_norm(DN_ALPHA * x + h, ln_g[i, 0], ln_b[i, 0])
        x = layer_norm(DN_ALPHA * x + swiglu(x, w_gate_up[i], w_down[i]), ln_g[i, 1], ln_b[i, 1])
    return x
```

```python
import contextlib
import numpy as np
import concourse.bass as bass
import concourse.mybir as mybir
from concourse.bass_utils import run_bass_kernel_spmd

F32 = mybir.dt.float32
BF16 = mybir.dt.bfloat16
I32 = mybir.dt.int32
ALU = mybir.AluOpType
AF = mybir.ActivationFunctionType
AX = mybir.AxisListType

D = 1024
SEQ = 4096
DEPTH = 4
DFF = 2816
NF = 22
ALPHA = float((2 * DEPTH) ** 0.25)
BETA = (8 * DEPTH) ** -0.25
EPS = 1e-5
KINDS = [0, 1, 2, 0]
NTL = [28, 84, 28, 28]
TWO_PI = float(2 * np.pi)
CW1 = 6.28125
CW2 = float(2 * np.pi - 6.28125)


class Buf:
    __slots__ = ("name", "w", "r")

    def __init__(self, name=""):
        self.name = name
        self.w = None
        self.r = []


class Sched:
    ENG = ("pe", "act", "dve", "pool", "sp")
    SEM_ROLL = 30000
    NDMA = 8

    def __init__(self, nc, stack):
        self.nc = nc
        self.stack = stack
        self._k = 0
        self.ops = {e: [] for e in self.ENG}
        self.cur_sem = {}
        self.cnt = {}
        for e in self.ENG:
            self._new_sem(e)
        self.dma_sems = {}
        self.dma_idx = {}
        self.waited = {e: {} for e in self.ENG}

    def _alloc(self, name):
        return self.stack.enter_context(self.nc.semaphore(name))

    def _new_sem(self, e):
        self._k += 1
        self.cur_sem[e] = self._alloc(f"s_{e}_{self._k}")
        self.cnt[e] = 0

    def _filter(self, eng, toks):
        need = {}
        wd = self.waited[eng]
        for (s, v) in toks:
            if wd.get(id(s), (None, -1))[1] >= v:
                continue
            if id(s) not in need or need[id(s)][1] < v:
                need[id(s)] = (s, v)
        for k, (s, v) in need.items():
            wd[k] = (s, v)
        return list(need.values())

    def _deps(self, eng, reads, writes):
        toks = []
        for b in reads:
            if b.w is not None:
                toks.append(b.w)
        for b in writes:
            if b.w is not None:
                toks.append(b.w)
            toks.extend(b.r)
        return self._filter(eng, toks)

    def _commit(self, tok, reads, writes):
        for b in reads:
            b.r.append(tok)
            if len(b.r) > 64:
                b.r = b.r[-64:]
        for b in writes:
            b.w = tok
            b.r = []

    def op(self, eng, fn, reads=(), writes=()):
        waits = self._deps(eng, reads, writes)
        if self.cnt[eng] >= self.SEM_ROLL:
            self._new_sem(eng)
        sem = self.cur_sem[eng]
        self.cnt[eng] += 1
        tok = (sem, self.cnt[eng])
        self.ops[eng].append((waits, fn, sem, 1))
        self._commit(tok, reads, writes)
        return tok

    def chain(self, eng, fns, reads=(), writes=()):
        tok = None
        for fn in fns:
            tok = self.op(eng, fn, reads, writes)
        return tok

    def dma(self, q, fn, reads=(), writes=(), nb=False):
        key = (q, "nb") if nb else q
        if key not in self.dma_sems:
            self.dma_sems[key] = [[self._alloc(f"d_{q}_{int(nb)}_{i}"), 0] for i in range(self.NDMA)]
            self.dma_idx[key] = 0
        waits = self._deps(q, reads, writes)
        i = self.dma_idx[key]
        self.dma_idx[key] = (i + 1) % self.NDMA
        ent = self.dma_sems[key][i]
        if ent[1] + 16 > 60000:
            self._k += 1
            ent[0] = self._alloc(f"d_{q}_{i}_r{self._k}")
            ent[1] = 0
        sem, val = ent
        if val > 0:
            waits = waits + self._filter(q, [(sem, val)])
        ent[1] = val + 16
        tok = (sem, val + 16)
        self.ops[q].append((waits, fn, sem, 16))
        self._commit(tok, reads, writes)
        return tok

    def all_tokens(self, include_nb=True):
        toks = [(self.cur_sem[e], self.cnt[e]) for e in self.ENG if self.cnt[e] > 0]
        for q in self.dma_sems:
            if isinstance(q, tuple) and not include_nb:
                continue
            toks += [(ent[0], ent[1]) for ent in self.dma_sems[q] if ent[1] > 0]
        return toks

    def barrier(self, include_nb=True):
        toks = self.all_tokens(include_nb)
        for e in self.ENG:
            w = self._filter(e, toks)
            if w:
                self.ops[e].append((w, None, None, 0))

    def emit(self, block):
        sch = self

        def mk(e):
            def body(engine):
                for (waits, fn, sem, inc) in sch.ops[e]:
                    for (s, v) in waits:
                        engine.wait_ge(s, v)
                    if fn is not None:
                        ins = fn()
                        ins.then_inc(sem, inc)
            return body
        block.tensor(mk("pe"))
        block.scalar(mk("act"))
        block.vector(mk("dve"))
        block.gpsimd(mk("pool"))
        block.sync(mk("sp"))


def build(n_layers=DEPTH, stop_after_att=False):
    nc = bass.Bass("TRN2", target_bir_lowering=False)
    ein = lambda n, s, d: nc.dram_tensor(n, s, d, kind="ExternalInput").ap()
    xT = ein("xT", [D, SEQ], F32)
    pos = ein("pos", [1, SEQ], I32)
    invf = ein("invf", [128, 1], F32)
    lng_d = ein("lng", [128, 64], F32)
    lnb_d = ein("lnb", [128, 64], F32)
    watt = [ein(f"watt{l}", [NTL[l], 128, 1024], F32) for l in range(DEPTH)]
    wo_d = ein("wo", [DEPTH, 8, 128, 1024], F32)
    wgu_d = ein("wgu", [DEPTH, 44, 128, 1024], F32)
    wd_d = ein("wd", [DEPTH, 8, 128, NF * 128], F32)
    batt_d = ein("batt", [128, 56], F32)
    bv_d = ein("bv", [2, 8, 128], F32)
    bo_d = ein("bo", [128, 16], F32)
    sinks_d = ein("sinks", [1, 32], F32)
    outT = nc.dram_tensor("outT", [D, SEQ], F32, kind="ExternalOutput").ap()
    Xs = nc.dram_tensor("Xs", [D, SEQ], F32).ap()
    XB = nc.dram_tensor("XBs", [D, SEQ], BF16).ap()
    OT = nc.dram_tensor("OTs", [D, SEQ], BF16).ap()
    ROPE = nc.dram_tensor("ROPEs", [12, 128, SEQ], BF16).ap()
    TAB = nc.dram_tensor("TABs", [2, 128, SEQ], F32).ap()

    st = contextlib.ExitStack()
    with st:
        ARENA_B = 212800
        arena = st.enter_context(nc.sbuf_tensor("arena", [128, ARENA_B // 2], BF16))
        PS = [st.enter_context(nc.psum_tensor(f"ps{i}", [128, 512], F32)) for i in range(8)]
        PB = [Buf(f"ps{i}") for i in range(8)]
        blk = st.enter_context(nc.Block())
        S = Sched(nc, st)

        class Al:
            off = 0

        def take(nbytes):
            o = Al.off
            Al.off += (nbytes + 63) // 64 * 64
            assert Al.off <= ARENA_B, Al.off
            return o

        def V(off, nbytes, dt=BF16):
            v = arena[:, off // 2:(off + nbytes) // 2]
            if dt != BF16:
                v = v.bitcast(dt)
            return v

        o_mask = take(2 * 1024)
        maskA = V(o_mask, 1024)
        maskB = V(o_mask + 1024, 1024)
        ones32 = V(take(256), 256, F32)
        onesb = V(take(256), 256)
        ident = V(take(256), 256)
        lng = V(take(256), 256, F32)
        lnb = V(take(256), 256, F32)
        batt = V(take(224), 224, F32)
        bo = V(take(64), 64, F32)
        esk = V(take(128), 128, F32)
        invf_t = V(take(64), 4, F32)
        bvt = V(take(512), 512, F32)
        B_const = Buf("const")
        B_bvt = Buf("bvt")
        base_off = Al.off

        Al.off = base_off
        o_xb = take(64 * 1024)
        xb = V(o_xb, 64 * 1024).rearrange("p (c t) -> p c t", c=8)
        B_xb = Buf("xb")
        wsl = []
        for i in range(2):
            o = take(6 * 1024)
            wsl.append((V(o, 6 * 1024).rearrange("p (t c j) -> p t c j", t=3, c=8), Buf(f"wsl{i}")))
        QKV = []

        def mk_qkv(o):
            q = V(o, 8192)
            ka = V(o + 8192, 8192)
            kb = V(o + 16384, 8192)
            va = V(o + 24576, 8320).rearrange("p (t h e) -> p t h e", t=32, h=2)
            return (q, ka, kb, va, Buf("QT"), Buf("K"), Buf("V"))
        QKV.append(mk_qkv(take(24576 + 8320)))
        o_oacc = take(32 * 1024)
        oacc = V(o_oacc, 32 * 1024, F32).rearrange("p (h t) -> p h t", h=2)
        B_oacc = Buf("oacc")
        COS = V(o_oacc, 16 * 1024, F32)
        SIN = V(o_oacc + 16 * 1024, 16 * 1024, F32)
        o_onb = take(16 * 1024)
        onb = V(o_onb, 16 * 1024).rearrange("p (h t) -> p h t", h=2)
        B_onb = Buf("onb")
        RotF = [V(o_onb, 8192), V(o_onb + 8192, 8192)]
        B_rot = [Buf("rot0"), Buf("rot1")]
        Pt = []
        for i in range(4):
            Pt.append((V(take(1024), 1024), Buf(f"pt{i}")))
        nrm = [(V(take(2048), 2048, F32), Buf(f"nrm{i}")) for i in range(2)]
        o_q1 = take(24576 + 8320)
        QKV.append(mk_qkv(o_q1))
        Rf = []
        for i in range(4):
            Rf.append((V(o_q1 + i * 2048, 2048, F32), Buf(f"rf{i}")))
        tt_ = [(V(o_q1 + 8192 + i * 2048, 2048, F32), Buf(f"tt{i}")) for i in range(2)]
        att_end = Al.off

        accC = V(o_oacc, 16640, F32).rearrange("p (t h e) -> p t h e", t=32, h=2)
        SEL = V(o_oacc + 16640, 4096, F32).rearrange("p (t h n) -> p t h n", t=32, h=2)
        Gt = V(o_oacc + 20736, 128, F32).rearrange("p (h n) -> p h n", h=2)
        top8 = V(o_oacc + 20864, 64, F32).rearrange("p (h n) -> p h n", h=2)
        kmf = V(o_oacc + 20928, 128, F32).rearrange("p (h n) -> p h n", h=2)
        kmh = V(o_oacc + 21056, 64).rearrange("p (h n) -> p h n", h=2)
        kml = V(o_oacc + 21120, 64).rearrange("p (h n) -> p h n", h=2)
        kmr = V(o_oacc + 21184, 128, F32).rearrange("p (h n) -> p h n", h=2)
        rcpC = V(o_oacc + 21312, 256, F32).rearrange("p (t h) -> p t h", t=32)
        obC = V(o_onb, 8192).rearrange("p (t e) -> p t e", t=32)
        onT = V(o_onb + 8192, 8192)
        B_sel, B_g, B_km, B_obC, B_onT = Buf("sel"), Buf("g"), Buf("km"), Buf("obC"), Buf("onT")

        Al.off = base_off
        Wgu = V(take(88 * 1024), 88 * 1024).rearrange("p (f n) -> p f n", f=44)
        B_wgu = [Buf(f"wgu{i}") for i in range(11)]
        wds = [(V(take(5632), 5632).rearrange("p (f j) -> p f j", f=NF), Buf(f"wds{i}")) for i in range(4)]
        wos = []
        for i in range(2):
            wos.append((V(take(2048), 2048), Buf(f"wos{i}")))
        TP = 512
        ZT = [V(take(16384), 16384, F32).rearrange("p (c t) -> p c t", c=8) for i in range(2)]
        BZ = [[Buf(f"z{i}_{m}") for m in range(8)] for i in range(2)]
        oTc = V(take(8192), 8192).rearrange("p (c t) -> p c t", c=8)
        B_oTc = Buf("oTc")
        x1b = V(take(8192), 8192).rearrange("p (c t) -> p c t", c=8)
        B_x1b = Buf("x1b")
        aT = V(take(22 * 1024), 22 * 1024).rearrange("p (f t) -> p f t", f=NF)
        B_aT = [Buf(f"aT{f}") for f in range(NF)]
        mean_sb = V(take(2048), 2048, F32)
        rstd_sb = V(take(2048), 2048, F32)
        B_stat = Buf("stat")
        zb = [(V(take(1024), 1024), Buf(f"zb{i}")) for i in range(4)]
        zsq = [(V(take(1024), 1024), Buf(f"zsq{i}")) for i in range(4)]
        lt = [(V(take(2048), 2048, F32), Buf(f"lt{i}")) for i in range(2)]
        sg = [(V(take(1024), 1024), Buf(f"sg{i}")) for i in range(2)]
        post_end = Al.off
        assert max(att_end, post_end) <= ARENA_B

        sp, act, dve, pool, pe = nc.sync, nc.scalar, nc.vector, nc.gpsimd, nc.tensor

        mA = maskA.rearrange("p (s i) -> p s i", s=4)
        mB = maskB.rearrange("p (s i) -> p s i", s=4)
        cfns = [lambda: pool.memset(arena[:, o_mask // 2:(o_mask + 2048) // 2], 1.0)]
        for s_ in range(2):
            cfns.append(lambda s_=s_: pool.affine_select(out=mA[:, s_, :], in_=mA[:, s_, :], pattern=[[-1, 128]],
                        compare_op=ALU.is_ge, fill=0.0, base=-1, channel_multiplier=1))
            cfns.append(lambda s_=s_: pool.affine_select(out=mB[:, s_, :], in_=mB[:, s_, :], pattern=[[-1, 128]],
                        compare_op=ALU.is_ge, fill=0.0, base=0, channel_multiplier=1))
        for s_ in range(2, 4):
            for mm_ in (mA, mB):
                cfns.append(lambda s_=s_, mm_=mm_: pool.affine_select(out=mm_[:, s_, :], in_=mm_[:, s_, :],
                            pattern=[[1, 128]], compare_op=ALU.is_ge, fill=0.0, base=0, channel_multiplier=-1))
        cfns.append(lambda: pool.memset(ones32, 1.0))
        cfns.append(lambda: pool.memset(onesb, 1.0 / 1024.0))
        cfns.append(lambda: pool.memset(ident, 1.0))
        cfns.append(lambda: pool.affine_select(out=ident, in_=ident, pattern=[[1, 128]],
                                               compare_op=ALU.is_equal, fill=0.0, base=0, channel_multiplier=-1))
        S.chain("pool", cfns, writes=[B_const])
        B_small = Buf("small")
        for dst, src in ((lng, lng_d), (lnb, lnb_d), (batt, batt_d), (bo, bo_d), (invf_t, invf)):
            S.dma("sp", lambda dst=dst, src=src: sp.dma_start(out=dst, in_=src), writes=[B_small])
        S.dma("sp", lambda: sp.dma_start(out=esk, in_=sinks_d.broadcast_to([128, 32])), writes=[B_small])
        S.op("act", lambda: act.activation(out=esk, in_=esk, func=AF.Exp), reads=[B_small], writes=[B_small])

        posi = V(o_xb, 16384, I32)
        ang = V(o_xb + 16384, 16384, F32)
        kf = V(o_xb + 32768, 16384, F32)
        ki = V(o_xb + 49152, 16384, I32)
        B_tab = Buf("tab")
        S.dma("sp", lambda: sp.dma_start(out=posi, in_=pos.broadcast_to([128, SEQ])), writes=[B_tab])

        S.chain("dve", [
            lambda: dve.tensor_copy(out=ang, in_=posi),
            lambda: dve.tensor_scalar(out=ang, in0=ang, scalar1=invf_t[:, 0:1], scalar2=None, op0=ALU.mult),
            lambda: dve.tensor_scalar(out=kf, in0=ang, scalar1=float(1.0 / TWO_PI), scalar2=None, op0=ALU.mult),
            lambda: dve.tensor_copy(out=ki, in_=kf),
            lambda: dve.tensor_copy(out=kf, in_=ki),
            lambda: dve.scalar_tensor_tensor(out=ang, in0=kf, scalar=-CW1, in1=ang, op0=ALU.mult, op1=ALU.add),
            lambda: dve.scalar_tensor_tensor(out=ang, in0=kf, scalar=-CW2, in1=ang, op0=ALU.mult, op1=ALU.add),
            lambda: dve.tensor_scalar(out=ang, in0=ang, scalar1=float(np.pi), scalar2=float(-np.pi),
                                      op0=ALU.min, op1=ALU.max),
            lambda: dve.scalar_tensor_tensor(out=kf, in0=ang, scalar=-1.0, in1=ang, op0=ALU.mult, op1=ALU.max),
            lambda: dve.tensor_scalar(out=kf, in0=kf, scalar1=-1.0, scalar2=float(np.pi / 2),
                                      op0=ALU.mult, op1=ALU.add),
        ], reads=[B_small], writes=[B_tab])
        S.op("act", lambda: act.activation(out=ang, in_=ang, func=AF.Sin), writes=[B_tab])
        S.op("act", lambda: act.activation(out=kf, in_=kf, func=AF.Sin), writes=[B_tab])
        S.dma("sp", lambda: sp.dma_start(out=TAB[1], in_=ang), reads=[B_tab])
        S.dma("sp", lambda: sp.dma_start(out=TAB[0], in_=kf), reads=[B_tab])
        S.barrier()
        xTv = xT.rearrange("(c p) t -> p c t", p=128)
        for c in range(8):
            for h_ in range(2):
                S.dma("pool", lambda c=c, h_=h_: pool.dma_start(
                    out=xb[:, c, h_ * 2048:(h_ + 1) * 2048], in_=xTv[:, c, h_ * 2048:(h_ + 1) * 2048]),
                    writes=[B_xb])

        pcnt = [0]

        def proj_bank():
            pcnt[0] += 1
            return pcnt[0] % 2

        def load_w(l, t0, nt):
            slot, bslot = wsl[load_w.i % 2]
            load_w.i += 1
            src = watt[l][t0:t0 + nt].rearrange("t p n -> p t n")
            dstv = slot[:, 0:nt].rearrange("p t c j -> p t (c j)")
            S.dma("pool", lambda: pool.dma_start(out=dstv, in_=src), writes=[bslot])
            return slot, bslot
        load_w.i = 0

        def phase_view(ap2d, d):
            return ap2d.rearrange("p (j r) -> p r j", r=d)

        def att_common_start(l):
            if l > 0:
                Xv_ = Xs.rearrange("(c p) t -> p c t", p=128)
                for c in range(8):
                    for h_ in range(2):
                        S.dma("pool", lambda c=c, h_=h_: pool.dma_start(
                            out=xb[:, c, h_ * 2048:(h_ + 1) * 2048], in_=Xv_[:, c, h_ * 2048:(h_ + 1) * 2048]),
                            writes=[B_xb])
            S.dma("sp", lambda: sp.dma_start(out=COS, in_=TAB[0]), writes=[B_oacc])
            S.dma("sp", lambda: sp.dma_start(out=SIN, in_=TAB[1]), writes=[B_oacc])

        def rope_prepass(l, groups, la):
            for g, (d, t_base) in enumerate(groups):
                for qk in range(2):
                    t0 = t_base + 2 * qk
                    slot, bslot = load_w(l, t0, 2)
                    L = SEQ // d
                    for tc in range(8):
                        rfs = []
                        for half in range(2):
                            b = proj_bank()
                            rf, brf = Rf[(tc % 2) * 2 + half]

                            def mmf(b=b, half=half, tc=tc, slot=slot):
                                for c in range(8):
                                    ins = pe.matmul(PS[b][:, :], lhsT=slot[:, half, c, :],
                                                    rhs=xb[:, c, tc * 512:(tc + 1) * 512],
                                                    start=(c == 0), stop=(c == 7))
                                return ins
                            S.op("pe", mmf, reads=[bslot, B_xb], writes=[PB[b]])
                            if la is not None:
                                bias_ap = batt[:, la * 28 + t0 + half:la * 28 + t0 + half + 1]
                                S.op("act", lambda b=b, rf=rf, bias_ap=bias_ap: act.activation(
                                    out=rf, in_=PS[b][:, :], func=AF.Identity, bias=bias_ap),
                                    reads=[PB[b], B_small], writes=[brf])
                            else:
                                S.op("act", lambda b=b, rf=rf: act.copy(out=rf, in_=PS[b][:, :]),
                                     reads=[PB[b]], writes=[brf])
                            rfs.append((rf, brf))
                        (r1, b1), (r2, b2) = rfs
                        (ta, bta), (tb, btb) = tt_
                        cs = COS[:, tc * 512:(tc + 1) * 512]
                        sn = SIN[:, tc * 512:(tc + 1) * 512]
                        w = 512 // d
                        o1 = RotF[0].rearrange("p (r j) -> p r j", r=d)[:, :, tc * w:(tc + 1) * w]
                        o2 = RotF[1].rearrange("p (r j) -> p r j", r=d)[:, :, tc * w:(tc + 1) * w]

                        pv_ = lambda a, d=d: phase_view(a, d)
                        S.chain("dve", [
                            lambda r1=r1, ta=ta, cs=cs: dve.tensor_tensor(out=ta, in0=r1, in1=cs, op=ALU.mult),
                            lambda r2=r2, tb=tb, sn=sn: dve.tensor_tensor(out=tb, in0=r2, in1=sn, op=ALU.mult),
                            lambda o1=o1, ta=ta, tb=tb, pv_=pv_: dve.tensor_tensor(out=o1, in0=pv_(ta), in1=pv_(tb),
                                                                                  op=ALU.subtract),
                            lambda r2=r2, ta=ta, cs=cs: dve.tensor_tensor(out=ta, in0=r2, in1=cs, op=ALU.mult),
                            lambda r1=r1, tb=tb, sn=sn: dve.tensor_tensor(out=tb, in0=r1, in1=sn, op=ALU.mult),
                            lambda o2=o2, ta=ta, tb=tb, pv_=pv_: dve.tensor_tensor(out=o2, in0=pv_(ta), in1=pv_(tb),
                                                                                  op=ALU.add),
                        ], reads=[b1, b2, B_oacc], writes=[bta, btb, B_rot[0], B_rot[1]])
                    for half in range(2):
                        idx = (g * 2 + qk) * 2 + half
                        S.dma("sp", lambda idx=idx, half=half: sp.dma_start(out=ROPE[idx], in_=RotF[half]),
                              reads=[B_rot[half]])

        def project_pair(l, p, g, d, t_base, la, bs):
            QT, KA, KB, Vaug, B_Q, B_K, B_V = QKV[bs]
            KK = [KA, KB]
            t0 = t_base + 4 + 3 * p
            slot, bslot = load_w(l, t0, 3)
            w = 512 // d
            QTv = QT.rearrange("p (r j) -> p r j", r=d)
            KAv = KA.rearrange("p (r j) -> p r j", r=d)
            KBv = KB.rearrange("p (r j) -> p r j", r=d)
            for tc in range(8):
                b = proj_bank()

                def mmq(b=b, tc=tc):
                    for c in range(8):
                        ins = pe.matmul(PS[b][:, :], lhsT=slot[:, 0, c, :], rhs=xb[:, c, tc * 512:(tc + 1) * 512],
                                        start=(c == 0), stop=(c == 7))
                    return ins
                S.op("pe", mmq, reads=[bslot, B_xb], writes=[PB[b]])
                qo = QTv[:, :, tc * w:(tc + 1) * w]
                if la is not None:
                    bq = batt[:, la * 28 + t0:la * 28 + t0 + 1]
                    S.op("act", lambda b=b, qo=qo, bq=bq: act.activation(
                        out=qo, in_=phase_view(PS[b][:, :], d), func=AF.Identity, bias=bq),
                        reads=[PB[b], B_small], writes=[B_Q])
                else:
                    S.op("act", lambda b=b, qo=qo: act.copy(out=qo, in_=phase_view(PS[b][:, :], d)),
                         reads=[PB[b]], writes=[B_Q])
                b = proj_bank()

                def mmk(b=b, tc=tc):
                    for c in range(8):
                        ins = pe.matmul(PS[b][:, :], lhsT=slot[:, 1, c, :], rhs=xb[:, c, tc * 512:(tc + 1) * 512],
                                        start=(c == 0), stop=(c == 7))
                    return ins
                S.op("pe", mmk, reads=[bslot, B_xb], writes=[PB[b]])
                ka_o = KAv[0:64, :, tc * w:(tc + 1) * w]
                kb_o = KBv[64:128, :, tc * w:(tc + 1) * w]
                if la is not None:
                    bk = batt[:, la * 28 + t0 + 1:la * 28 + t0 + 2]
                    S.op("act", lambda b=b, ka_o=ka_o, bk=bk: act.activation(
                        out=ka_o, in_=phase_view(PS[b][0:64, :], d), func=AF.Identity, bias=bk[0:64, :]),
                        reads=[PB[b], B_small], writes=[B_K])
                    S.op("dve", lambda b=b, kb_o=kb_o, bk=bk: dve.tensor_scalar(
                        out=kb_o, in0=phase_view(PS[b][64:128, :], d), scalar1=bk[64:128, :], scalar2=None,
                        op0=ALU.add), reads=[PB[b], B_small], writes=[B_K])
                else:
                    S.op("act", lambda b=b, ka_o=ka_o: act.copy(out=ka_o, in_=phase_view(PS[b][0:64, :], d)),
                         reads=[PB[b]], writes=[B_K])
                    S.op("dve", lambda b=b, kb_o=kb_o: dve.tensor_copy(out=kb_o, in_=phase_view(PS[b][64:128, :], d)),
                         reads=[PB[b]], writes=[B_K])
                yield
            nbp = 32 // d
            if la is not None:
                S.dma("sp", lambda: sp.dma_start(out=bvt, in_=bv_d[la, p:p + 1, :].broadcast_to([128, 128])),
                      writes=[B_bvt])
            for t4 in range(8):
                b = proj_bank()

                def mmv(b=b, t4=t4):
                    for j in range(4):
                        tt = t4 * 4 + j
                        r, n = divmod(tt, nbp)
                        s0 = r + d * 128 * n
                        for c in range(8):
                            ins = pe.matmul(PS[b][:, j * 128:(j + 1) * 128],
                                            lhsT=xb[:, c, s0:s0 + d * 127 + 1:d], rhs=slot[:, 2, c, :],
                                            start=(c == 0), stop=(c == 7))
                    return ins
                S.op("pe", mmv, reads=[bslot, B_xb], writes=[PB[b]])
                vo = Vaug[:, t4 * 4:(t4 + 1) * 4, :, 0:64]
                pv = PS[b][:, :].rearrange("p (t h e) -> p t h e", t=4, h=2)
                if la is not None:
                    bvv = bvt.rearrange("p (h e) -> p h e", h=2).unsqueeze(1).broadcast_to([128, 4, 2, 64])
                    S.op("dve", lambda vo=vo, pv=pv, bvv=bvv: dve.tensor_tensor(out=vo, in0=pv, in1=bvv, op=ALU.add),
                         reads=[PB[b], B_bvt], writes=[B_V])
                else:
                    S.op("dve", lambda vo=vo, pv=pv: dve.tensor_copy(out=vo, in_=pv), reads=[PB[b]], writes=[B_V])
                yield
            for hh in range(2):
                h = 2 * p + hh
                for half in range(2):
                    iq = (g * 2 + 0) * 2 + half
                    ik = (g * 2 + 1) * 2 + half
                    r0 = hh * 64 + half * 8
                    S.dma("sp", lambda iq=iq, r0=r0, h=h: sp.dma_start(
                        out=QT[r0:r0 + 8, :], in_=ROPE[iq][h * 8:(h + 1) * 8, :]), writes=[B_Q])
                    S.dma("sp", lambda ik=ik, r0=r0, h=h, hh=hh: sp.dma_start(
                        out=KK[hh][r0:r0 + 8, :], in_=ROPE[ik][h * 8:(h + 1) * 8, :]), writes=[B_K])
            yield

        def att_init_kv():
            for bs in range(2):
                QT, KA, KB, Vaug, B_Q, B_K, B_V = QKV[bs]

                def f(KA=KA, KB=KB, Vaug=Vaug):
                    pool.memset(KA[64:128, :], 0.0)
                    pool.memset(KB[0:64, :], 0.0)
                    return pool.memset(Vaug[:, :, :, 64:65], 1.0)
                S.op("pool", f, writes=[B_K, B_V])

        def banded_attention(d, mask, first, bs):
            QT, KA, KB, Vaug, B_Q, B_K, B_V = QKV[bs]
            KK = [KA, KB]
            nbp = 32 // d

            SBK = [2, 3, 6, 7]

            def s_stage(qt):
                r, n = divmod(qt, nbp)
                hp = n > 0
                sb = SBK[qt % 4]

                def f(qt=qt, hp=hp, sb=sb):
                    q = QT[:, qt * 128:(qt + 1) * 128]
                    ins = None
                    for hh in range(2):
                        if hp:
                            pe.matmul(PS[sb][:, hh * 128:(hh + 1) * 128], lhsT=KK[hh][:, (qt - 1) * 128:qt * 128],
                                      rhs=q, start=True, stop=True)
                        ins = pe.matmul(PS[sb][:, (2 + hh) * 128:(3 + hh) * 128],
                                        lhsT=KK[hh][:, qt * 128:(qt + 1) * 128], rhs=q, start=True, stop=True)
                    return ins
                S.op("pe", f, reads=[B_Q, B_K], writes=[PB[sb]])

            def rest(qt):
                r, n = divmod(qt, nbp)
                hp = n > 0
                sb = SBK[qt % 4]
                ob = 4 + qt % 2
                pt, bpt = Pt[qt % 4]
                lo = 0 if hp else 256
                S.op("act", lambda: act.activation(out=pt[:, lo:512], in_=PS[sb][:, lo:512], func=AF.Exp,
                                                   scale=0.125), reads=[PB[sb]], writes=[bpt])
                S.op("pool", lambda: pool.tensor_tensor(out=pt[:, lo:512], in0=pt[:, lo:512], in1=mask[:, lo:512],
                                                        op=ALU.mult), reads=[B_const], writes=[bpt])

                def pv():
                    ins = None
                    for hh in range(2):
                        o = PS[ob][0:65, hh * 128:(hh + 1) * 128]
                        if hp:
                            pe.matmul(o, lhsT=Vaug[:, qt - 1, hh, :], rhs=pt[:, hh * 128:(hh + 1) * 128],
                                      start=True, stop=False)
                        ins = pe.matmul(o, lhsT=Vaug[:, qt, hh, :], rhs=pt[:, (2 + hh) * 128:(3 + hh) * 128],
                                        start=(not hp), stop=True)
                    return ins
                S.op("pe", pv, reads=[bpt, B_V], writes=[PB[ob]])
                s0 = r + d * 128 * n
                dst = oacc[0:65, :, s0:s0 + d * 127 + 1:d]
                src = PS[ob][0:65, 0:256].rearrange("p (h q) -> p h q", h=2)
                if first:
                    S.op("dve", lambda: dve.tensor_copy(out=dst, in_=src), reads=[PB[ob]], writes=[B_oacc])
                else:
                    S.op("dve", lambda: dve.tensor_tensor(out=dst, in0=dst, in1=src, op=ALU.add),
                         reads=[PB[ob]], writes=[B_oacc])
            s_stage(0)
            s_stage(1)
            s_stage(2)
            for qt in range(32):
                if qt + 3 < 32:
                    s_stage(qt + 3)
                rest(qt)
                yield

        def normalize_store(p, la):
            k = 0
            for hh in range(2):
                h = 2 * p + hh
                for tc in range(8):
                    b = 6 + k % 2
                    nb_, bnb = nrm[k % 2]
                    k += 1
                    S.op("pe", lambda b=b, hh=hh, tc=tc: pe.matmul(
                        PS[b][0:64, :], lhsT=ones32[64:65, 0:64], rhs=oacc[64:65, hh, tc * 512:(tc + 1) * 512],
                        start=True, stop=True), reads=[B_oacc, B_const], writes=[PB[b]])
                    if la is not None:
                        sk = esk[0:64, la * 16 + h:la * 16 + h + 1]
                        S.op("act", lambda b=b, nb_=nb_, sk=sk: act.activation(
                            out=nb_[0:64, :], in_=PS[b][0:64, :], func=AF.Ln, bias=sk),
                            reads=[PB[b], B_small], writes=[bnb])
                    else:
                        S.op("act", lambda b=b, nb_=nb_: act.activation(
                            out=nb_[0:64, :], in_=PS[b][0:64, :], func=AF.Ln), reads=[PB[b]], writes=[bnb])
                    S.op("act", lambda nb_=nb_: act.activation(out=nb_[0:64, :], in_=nb_[0:64, :], func=AF.Exp,
                                                               scale=-1.0), writes=[bnb])
                    S.op("dve", lambda hh=hh, tc=tc, nb_=nb_: dve.tensor_tensor(
                        out=onb[0:64, hh, tc * 512:(tc + 1) * 512], in0=oacc[0:64, hh, tc * 512:(tc + 1) * 512],
                        in1=nb_[0:64, :], op=ALU.mult), reads=[bnb, B_oacc], writes=[B_onb])
            for hh in range(2):
                h = 2 * p + hh
                S.dma("sp", lambda hh=hh, h=h: sp.dma_start(out=OT[h * 64:(h + 1) * 64, :], in_=onb[0:64, hh, :]),
                      reads=[B_onb])

        MOBA_SB = [2, 3, 6, 7]

        def moba_pair(p, bs):
            QT, KA, KB, Vaug, B_Q, B_K, B_V = QKV[bs]
            KK = [KA, KB]
            S.op("pool", lambda: pool.memset(accC, 0.0), writes=[B_oacc])
            S.chain("dve", [
                lambda: dve.tensor_reduce(out=kmf[:, 0, :], in_=KK[0].rearrange("p (n t) -> p n t", n=16),
                                          axis=AX.X, op=ALU.add),
                lambda: dve.tensor_reduce(out=kmf[:, 1, :], in_=KK[1].rearrange("p (n t) -> p n t", n=16),
                                          axis=AX.X, op=ALU.add),
                lambda: dve.tensor_scalar(out=kmf, in0=kmf, scalar1=1.0 / 256.0, scalar2=None, op0=ALU.mult),
                lambda: dve.tensor_copy(out=kmh, in_=kmf),
                lambda: dve.tensor_copy(out=kmr, in_=kmh),
                lambda: dve.tensor_tensor(out=kmr, in0=kmf, in1=kmr, op=ALU.subtract),
                lambda: dve.tensor_copy(out=kml, in_=kmr),
            ], reads=[B_K], writes=[B_km])
            gb = 6
            for qt in range(2, 32):
                npast = qt // 2

                def gmm(qt=qt):
                    q = QT[:, qt * 128:(qt + 1) * 128]
                    ins = None
                    for hh in range(2):
                        pe.matmul(PS[gb][:, hh * 16:(hh + 1) * 16], lhsT=q, rhs=kmh[:, hh, :], start=True, stop=False)
                        ins = pe.matmul(PS[gb][:, hh * 16:(hh + 1) * 16], lhsT=q, rhs=kml[:, hh, :],
                                        start=False, stop=True)
                    return ins
                S.op("pe", gmm, reads=[B_Q, B_km], writes=[PB[gb]])
                S.chain("dve", [
                    lambda: dve.memset(Gt, -1e30),
                    lambda npast=npast: dve.tensor_copy(
                        out=Gt[:, :, 0:npast],
                        in_=PS[gb][:, 0:32].rearrange("p (h n) -> p h n", h=2)[:, :, 0:npast]),
                    lambda: dve.max(out=top8[:, 0, :], in_=Gt[:, 0, :]),
                    lambda: dve.max(out=top8[:, 1, :], in_=Gt[:, 1, :]),
                    lambda qt=qt: dve.tensor_scalar(out=SEL[:, qt, 0, :], in0=Gt[:, 0, :], scalar1=top8[:, 0, 2:3],
                                                    scalar2=None, op0=ALU.is_ge),
                    lambda qt=qt: dve.tensor_scalar(out=SEL[:, qt, 1, :], in0=Gt[:, 1, :], scalar1=top8[:, 1, 2:3],
                                                    scalar2=None, op0=ALU.is_ge),
                ], reads=[PB[gb]], writes=[B_sel, B_g])
                if qt % 4 == 3:
                    yield

            cnt = [0]
            ocnt = [0]

            def s_exp(hh, kt, q0, nq, tri):
                i = cnt[0]
                cnt[0] += 1
                sb = MOBA_SB[i % 4]
                pt, bpt = Pt[i % 4]
                K = KK[hh]
                S.op("pe", lambda: pe.matmul(PS[sb][:, 0:nq], lhsT=K[:, kt * 128:(kt + 1) * 128],
                                             rhs=QT[:, q0:q0 + nq], start=True, stop=True),
                     reads=[B_Q, B_K], writes=[PB[sb]])
                S.op("act", lambda: act.activation(out=pt[:, 0:nq], in_=PS[sb][:, 0:nq], func=AF.Exp, scale=0.125),
                     reads=[PB[sb]], writes=[bpt])
                if tri:
                    S.op("pool", lambda: pool.tensor_tensor(out=pt[:, 0:128], in0=pt[:, 0:128], in1=maskA[:, 256:384],
                                                            op=ALU.mult), reads=[B_const], writes=[bpt])
                return pt, bpt

            def stage_a(item):
                kind, hh, qc, x = item
                if kind == "past":
                    n = x
                    return [s_exp(hh, 2 * n + j, qc * 512, 512, False) for j in range(2)]
                qt = qc * 4 + x
                if kind == "own":
                    kts = ([qt - 1] if qt % 2 == 1 else []) + [qt]
                    return [s_exp(hh, kt, qt * 128, 128, kt == qt) for kt in kts]
                n = 2 * qc
                return [s_exp(hh, 2 * n + j, qt * 128, 128, False) for j in range(2)]

            def stage_b(item, pts):
                kind, hh, qc, x = item
                ob = 4 + ocnt[0] % 2
                ocnt[0] += 1
                rd = [t[1] for t in pts] + [B_V]
                if kind == "past":
                    n = x

                    def pvf():
                        ins = None
                        for qi in range(4):
                            for j in range(2):
                                ins = pe.matmul(PS[ob][:, qi * 65:(qi + 1) * 65],
                                                lhsT=pts[j][0][:, qi * 128:(qi + 1) * 128],
                                                rhs=Vaug[:, 2 * n + j, hh, :], start=(j == 0), stop=(j == 1))
                        return ins
                    S.op("pe", pvf, reads=rd, writes=[PB[ob]])

                    def accf():
                        ins = None
                        for qi in range(4):
                            qt = qc * 4 + qi
                            a = accC[:, qt, hh, :]
                            ins = dve.scalar_tensor_tensor(out=a, in0=PS[ob][:, qi * 65:(qi + 1) * 65],
                                                           scalar=SEL[:, qt, hh, n:n + 1], in1=a,
                                                           op0=ALU.mult, op1=ALU.add)
                        return ins
                    S.op("dve", accf, reads=[PB[ob], B_sel], writes=[B_oacc])
                    return
                qt = qc * 4 + x
                a = accC[:, qt, hh, :]
                if kind == "own":
                    kts = ([qt - 1] if qt % 2 == 1 else []) + [qt]
                else:
                    n = 2 * qc
                    kts = [2 * n, 2 * n + 1]

                def pvo():
                    ins = None
                    for j, kt in enumerate(kts):
                        ins = pe.matmul(PS[ob][:, 0:65], lhsT=pts[j][0][:, 0:128], rhs=Vaug[:, kt, hh, :],
                                        start=(j == 0), stop=(j == len(kts) - 1))
                    return ins
                S.op("pe", pvo, reads=rd, writes=[PB[ob]])
                if kind == "own":
                    S.op("dve", lambda: dve.tensor_tensor(out=a, in0=a, in1=PS[ob][:, 0:65], op=ALU.add),
                         reads=[PB[ob]], writes=[B_oacc])
                else:
                    n = 2 * qc
                    S.op("dve", lambda: dve.scalar_tensor_tensor(
                        out=a, in0=PS[ob][:, 0:65], scalar=SEL[:, qt, hh, n:n + 1], in1=a,
                        op0=ALU.mult, op1=ALU.add), reads=[PB[ob], B_sel], writes=[B_oacc])

            items = []
            for hh in range(2):
                for qc in range(8):
                    for n in range(2 * qc):
                        items.append(("past", hh, qc, n))
                    for qi in range(4):
                        items.append(("own", hh, qc, qi))
                        if qi >= 2:
                            items.append(("gated", hh, qc, qi))
            cur = stage_a(items[0])
            for i, it in enumerate(items):
                nxt = stage_a(items[i + 1]) if i + 1 < len(items) else None
                stage_b(it, cur)
                cur = nxt
                if i % 2 == 1:
                    yield
            S.chain("dve", [
                lambda: dve.reciprocal(out=rcpC, in_=accC[:, :, :, 64]),
                lambda: dve.tensor_tensor(
                    out=obC.rearrange("p t (h e) -> p t h e", h=2), in0=accC[:, :, :, 0:64],
                    in1=rcpC.unsqueeze(3).broadcast_to([128, 32, 2, 64]), op=ALU.mult),
            ], reads=[B_oacc], writes=[B_obC, B_oacc])
            for t4 in range(8):
                tb = 7
                psb = PS[tb][:, :].bitcast(BF16)

                def trf(t4=t4, psb=psb):
                    ins = None
                    for j in range(4):
                        ins = pe.transpose(psb[:, j * 128:(j + 1) * 128], obC[:, t4 * 4 + j, :], ident)
                    return ins
                S.op("pe", trf, reads=[B_obC, B_const], writes=[PB[tb]])
                S.op("act", lambda t4=t4, psb=psb: act.copy(out=onT[:, t4 * 512:(t4 + 1) * 512], in_=psb[:, 0:512]),
                     reads=[PB[tb]], writes=[B_onT])
            S.dma("sp", lambda: sp.dma_start(out=OT[p * 128:(p + 1) * 128, :], in_=onT), reads=[B_onT])
            yield

        def run_all(gen):
            for _ in gen:
                pass

        def interleave(a, b):
            gens = [a, b]
            while any(g is not None for g in gens):
                for i in range(2):
                    if gens[i] is not None:
                        try:
                            next(gens[i])
                        except StopIteration:
                            gens[i] = None

        ctr = {"zb": 0, "lt": 0, "sg": 0, "ob": 0, "gu": 0, "wo": 0, "wd": 0}

        def layer_norm(lidx, zt, B_z, want_bf16):
            mb, eb = 0, 1
            pend = []

            def pe_stat(m, zbt, bzb, zst, bzs):
                S.op("pe", lambda: pe.matmul(PS[mb][:, :], lhsT=onesb, rhs=zbt, start=(m == 0), stop=(m == 7)),
                     reads=[bzb, B_const], writes=[PB[mb]])
                S.op("pe", lambda: pe.matmul(PS[eb][:, :], lhsT=onesb, rhs=zst, start=(m == 0), stop=(m == 7)),
                     reads=[bzs, B_const], writes=[PB[eb]])
            for m in range(8):
                i = ctr["zb"] % 4
                ctr["zb"] += 1
                zbt, bzb = zb[i]
                zst, bzs = zsq[i]
                S.op("act", lambda m=m, zbt=zbt: act.copy(out=zbt, in_=zt[:, m, :]), reads=[B_z[m]], writes=[bzb])
                S.op("act", lambda m=m, zst=zst: act.activation(out=zst, in_=zt[:, m, :], func=AF.Square),
                     reads=[B_z[m]], writes=[bzs])
                pend.append((m, zbt, bzb, zst, bzs))
                if len(pend) > 2:
                    pe_stat(*pend.pop(0))
                yield
            for pnd in pend:
                pe_stat(*pnd)
            yield
            S.chain("dve", [
                lambda: dve.tensor_copy(out=mean_sb, in_=PS[mb][:, :]),
                lambda: dve.tensor_tensor(out=rstd_sb, in0=mean_sb, in1=mean_sb, op=ALU.mult),
                lambda: dve.tensor_tensor(out=rstd_sb, in0=PS[eb][:, :], in1=rstd_sb, op=ALU.subtract),
                lambda: dve.tensor_scalar(out=rstd_sb, in0=rstd_sb, scalar1=EPS, scalar2=None, op0=ALU.add),
            ], reads=[PB[mb], PB[eb]], writes=[B_stat])
            S.op("act", lambda: act.activation(out=rstd_sb, in_=rstd_sb, func=AF.Ln), writes=[B_stat])
            S.op("act", lambda: act.activation(out=rstd_sb, in_=rstd_sb, func=AF.Exp, scale=-0.5), writes=[B_stat])
            yield
            for m in range(8):
                t_, bt_ = lt[ctr["lt"] % 2]
                ctr["lt"] += 1
                S.chain("dve", [
                    lambda m=m, t_=t_: dve.tensor_tensor(out=t_, in0=zt[:, m, :], in1=mean_sb, op=ALU.subtract),
                    lambda t_=t_: dve.tensor_tensor(out=t_, in0=t_, in1=rstd_sb, op=ALU.mult),
                ], reads=[B_z[m], B_stat], writes=[bt_])
                gi = lidx * 8 + m

                def aff(m=m, t_=t_, gi=gi):
                    ins = act.activation(out=zt[:, m, :], in_=t_, func=AF.Identity, scale=lng[:, gi:gi + 1],
                                         bias=lnb[:, gi:gi + 1])
                    if want_bf16:
                        ins = act.activation(out=x1b[:, m, :], in_=t_, func=AF.Identity, scale=lng[:, gi:gi + 1],
                                             bias=lnb[:, gi:gi + 1])
                    return ins
                S.op("act", aff, reads=[bt_, B_small], writes=[B_z[m]] + ([B_x1b] if want_bf16 else []))
                yield

        def post_phase(l, last):
            la = {0: 0, 3: 1}.get(l)
            for i in range(8, 11):
                src = wgu_d[l, i * 4:(i + 1) * 4].rearrange("t p n -> p t n")
                S.dma("pool", lambda i=i, src=src: pool.dma_start(out=Wgu[:, i * 4:(i + 1) * 4, :], in_=src),
                      writes=[B_wgu[i]])
            Xsrc = xT if l == 0 else Xs
            Xdst = outT if last else Xs
            Xsv = Xsrc.rearrange("(c p) t -> p c t", p=128)
            Xdv = Xdst.rearrange("(c p) t -> p c t", p=128)
            OTv = OT.rearrange("(c p) t -> p c t", p=128)
            NCH = SEQ // TP

            otc_done = set()

            def load_otc(i):
                if i in otc_done or i >= NCH:
                    return
                otc_done.add(i)
                tsl_ = slice(i * TP, (i + 1) * TP)
                S.dma("sp", lambda: sp.dma_start(out=oTc, in_=OTv[:, :, tsl_]), writes=[B_oTc])

            def gen_a(i):
                zt, B_z = ZT[i % 2], BZ[i % 2]
                tsl = slice(i * TP, (i + 1) * TP)
                load_otc(i)
                for m in range(8):
                    S.dma("sp", lambda m=m: sp.dma_start(out=zt[:, m, :], in_=Xsv[:, m, tsl]), writes=[B_z[m]])
                yield
                for m in range(8):
                    wt, bwt = wos[ctr["wo"] % 2]
                    ctr["wo"] += 1
                    S.dma("pool", lambda m=m, wt=wt: pool.dma_start(out=wt, in_=wo_d[l, m]), writes=[bwt])
                    b = 2 + ctr["ob"] % 2
                    ctr["ob"] += 1
                    wv = wt.rearrange("p (c j) -> p c j", c=8)

                    def mmo(b=b, wv=wv):
                        for c in range(8):
                            ins = pe.matmul(PS[b][:, :], lhsT=wv[:, c, :], rhs=oTc[:, c, :], start=(c == 0),
                                            stop=(c == 7))
                        return ins
                    S.op("pe", mmo, reads=[bwt, B_oTc], writes=[PB[b]])
                    rf_ = [lambda b=b, m=m: dve.scalar_tensor_tensor(out=zt[:, m, :], in0=zt[:, m, :], scalar=ALPHA,
                                                                     in1=PS[b][:, :], op0=ALU.mult, op1=ALU.add)]
                    if la is not None:
                        rf_.append(lambda m=m: dve.tensor_scalar(out=zt[:, m, :], in0=zt[:, m, :],
                                                                 scalar1=bo[:, la * 8 + m:la * 8 + m + 1],
                                                                 scalar2=None, op0=ALU.add))
                    S.chain("dve", rf_, reads=[PB[b], B_small], writes=[B_z[m]])
                    yield
                yield from layer_norm(l * 2 + 0, zt, B_z, True)

            wd_state = {"issued": {}}

            def wd_issue(i, m):
                if (i, m) in wd_state["issued"] or i >= NCH or m >= 8:
                    return
                wt, bwt = wds[ctr["wd"] % 4]
                ctr["wd"] += 1
                wd_state["issued"][(i, m)] = (wt, bwt)
                for h_ in range(2):
                    S.dma("pool", lambda h_=h_, wt=wt: pool.dma_start(
                        out=wt[:, h_ * 11:(h_ + 1) * 11, :].rearrange("p f j -> p (f j)"),
                        in_=wd_d[l, m, :, h_ * 1408:(h_ + 1) * 1408]), writes=[bwt])

            def gen_b(i):
                zt, B_z = ZT[i % 2], BZ[i % 2]
                for m in range(4):
                    wd_issue(i, m)
                for m in range(8):
                    wt, bwt = wd_state["issued"][(i, m)]
                    b = 2 + ctr["ob"] % 2
                    ctr["ob"] += 1

                    def mmd(b=b, wt=wt):
                        for f in range(NF):
                            ins = pe.matmul(PS[b][:, :], lhsT=wt[:, f, :], rhs=aT[:, f, :], start=(f == 0),
                                            stop=(f == NF - 1))
                        return ins
                    S.op("pe", mmd, reads=[bwt] + B_aT, writes=[PB[b]])
                    S.op("dve", lambda b=b, m=m: dve.scalar_tensor_tensor(
                        out=zt[:, m, :], in0=zt[:, m, :], scalar=ALPHA, in1=PS[b][:, :], op0=ALU.mult, op1=ALU.add),
                        reads=[PB[b]], writes=[B_z[m]])
                    wd_issue(i, m + 4)
                    yield

            def gen_c(i):
                for f in range(NF):
                    k = ctr["gu"] % 2
                    ctr["gu"] += 1
                    gbk = 4 + k
                    ubk = 6 + k

                    def mmg(f=f, gbk=gbk, ubk=ubk):
                        wg = Wgu[:, f, :].rearrange("p (c j) -> p c j", c=8)
                        wu = Wgu[:, NF + f, :].rearrange("p (c j) -> p c j", c=8)
                        for c in range(8):
                            pe.matmul(PS[gbk][:, :], lhsT=wg[:, c, :], rhs=x1b[:, c, :], start=(c == 0), stop=(c == 7))
                        for c in range(8):
                            ins = pe.matmul(PS[ubk][:, :], lhsT=wu[:, c, :], rhs=x1b[:, c, :], start=(c == 0),
                                            stop=(c == 7))
                        return ins
                    S.op("pe", mmg, reads=[B_wgu[f // 4], B_wgu[(NF + f) // 4], B_x1b], writes=[PB[gbk], PB[ubk]])
                    sgt, bsg = sg[ctr["sg"] % 2]
                    ctr["sg"] += 1
                    S.op("act", lambda gbk=gbk, sgt=sgt: act.activation(out=sgt, in_=PS[gbk][:, :], func=AF.Silu),
                         reads=[PB[gbk]], writes=[bsg])
                    S.op("dve", lambda f=f, ubk=ubk, sgt=sgt: dve.tensor_tensor(out=aT[:, f, :], in0=sgt,
                                                                                 in1=PS[ubk][:, :], op=ALU.mult),
                         reads=[PB[ubk], bsg], writes=[B_aT[f]])
                    if f in (8, 11, 14, 17):
                        wd_issue(i, (f - 8) // 3)
                    yield

            def gen_d(i):
                zt, B_z = ZT[i % 2], BZ[i % 2]
                tsl = slice(i * TP, (i + 1) * TP)
                yield from layer_norm(l * 2 + 1, zt, B_z, False)
                for m in range(8):
                    S.dma("sp", lambda m=m: sp.dma_start(out=Xdv[:, m, tsl], in_=zt[:, m, :]), reads=[B_z[m]])
                yield

            run_all(gen_a(0))
            run_all(gen_c(0))
            for i in range(NCH):
                if i + 1 < NCH:
                    ga = gen_a(i + 1)
                    for _ in range(9):
                        next(ga)
                    interleave(ga, gen_b(i))
                    load_otc(i + 2)
                    interleave(gen_c(i + 1), gen_d(i))
                else:
                    run_all(gen_b(i))
                    run_all(gen_d(i))

        for l in range(n_layers):
            kind = KINDS[l]
            la = {0: 0, 3: 1}.get(l)
            att_common_start(l)
            if kind == 1:
                groups = [(1, 0), (4, 28), (16, 56)]
                mask = maskB
            else:
                groups = [(1, 0)]
                mask = maskA
            rope_prepass(l, groups, la)
            S.barrier()
            att_init_kv()
            if kind == 2:
                S.op("pool", lambda: pool.memset(SEL, 0.0), writes=[B_sel])
                units = [(p, 0, 1, 0) for p in range(8)]
            else:
                units = [(p, g, d, tb_) for p in range(8) for g, (d, tb_) in enumerate(groups)]
            lac = None if kind == 2 else la
            run_all(project_pair(l, units[0][0], units[0][1], units[0][2], units[0][3], lac, 0))
            for ui, (p, g, d, tb_) in enumerate(units):
                bs = ui % 2
                if kind == 2:
                    att = moba_pair(p, bs)
                else:
                    att = banded_attention(d, mask, g == 0, bs)
                nxt = None
                if ui + 1 < len(units):
                    p2, g2, d2, tb2 = units[ui + 1]
                    nxt = project_pair(l, p2, g2, d2, tb2, lac, 1 - bs)
                if nxt is None:
                    for i in (0, 5, 6, 1, 7, 2, 3, 4):
                        src = wgu_d[l, i * 4:(i + 1) * 4].rearrange("t p n -> p t n")
                        S.dma("pool", lambda i=i, src=src: pool.dma_start(out=Wgu[:, i * 4:(i + 1) * 4, :], in_=src),
                              writes=[B_wgu[i], B_xb], nb=True)
                interleave(att, nxt)
                if kind != 2 and g == len(groups) - 1:
                    normalize_store(p, la)
            S.barrier(include_nb=False)
            if stop_after_att and l == n_layers - 1:
                break
            post_phase(l, last=(l == n_layers - 1))
            S.barrier()
        S.barrier()
        S.emit(blk)
    return nc


def _tile(W, cols):
    t = W[:, cols].reshape(8, 128, 128).transpose(1, 0, 2)
    return np.ascontiguousarray(t).reshape(128, 1024)


def _att_tiles(kind, W):
    tiles = []
    hd = np.arange(64)
    if kind == 0:
        qcol = lambda h: h * 64 + hd
        kcol = lambda h: 1024 + (h // 4) * 64 + hd
        vcol = lambda h: 1280 + (h // 4) * 64 + hd
        groups = [(qcol, kcol, vcol)]
    elif kind == 1:
        groups = []
        for g in range(3):
            groups.append((lambda h, g=g: ((g * 3 + 0) * 16 + h) * 64 + hd,
                           lambda h, g=g: ((g * 3 + 1) * 16 + h) * 64 + hd,
                           lambda h, g=g: ((g * 3 + 2) * 16 + h) * 64 + hd))
    else:
        groups = [(lambda h: (0 * 16 + h) * 64 + hd, lambda h: (1 * 16 + h) * 64 + hd,
                   lambda h: (2 * 16 + h) * 64 + hd)]
    for (qc, kc, vc) in groups:
        for fc in (qc, kc):
            for half in range(2):
                cols = np.concatenate([fc(h)[half * 8:(half + 1) * 8] for h in range(16)])
                tiles.append(cols)
        for p in range(8):
            for fc in (qc, kc, vc):
                tiles.append(np.concatenate([fc(2 * p), fc(2 * p + 1)]))
    return tiles


def _prep_common(inp):
    f32 = np.float32
    com = {}
    com["invf"] = np.tile((500000.0 ** (-np.arange(0, 16, 2, dtype=np.float32) / 16)).astype(f32), 16).reshape(128, 1)
    lg = np.asarray(inp["ln_g"], f32).reshape(8, 8, 128)
    lb = np.asarray(inp["ln_b"], f32).reshape(8, 8, 128)
    com["lng"] = np.ascontiguousarray(lg.transpose(2, 0, 1).reshape(128, 64))
    com["lnb"] = np.ascontiguousarray(lb.transpose(2, 0, 1).reshape(128, 64))
    srcs = [(0, inp["a_w_qkv"][0]), (1, inp["b_w_qkv"][0]), (2, inp["c_w_qkv"][0]), (0, inp["a_w_qkv"][1])]
    batt = np.zeros((128, 56), f32)
    bv = np.zeros((2, 8, 128), f32)
    for l, (kind, W) in enumerate(srcs):
        W = np.asarray(W, f32)
        cols = _att_tiles(kind, W)
        com[f"watt{l}"] = np.stack([_tile(W, c) for c in cols])
        if kind == 0:
            la = 0 if l == 0 else 1
            bq = np.asarray(inp["a_b_qkv"][la], f32)
            for t, c in enumerate(cols):
                batt[:, la * 28 + t] = bq[c]
            for p in range(8):
                bv[la, p] = bq[cols[4 + 3 * p + 2]]
    com["batt"] = batt
    com["bv"] = bv
    wos = [inp["a_w_o"][0], inp["b_w_o"][0], inp["c_w_o"][0], inp["a_w_o"][1]]
    com["wo"] = np.stack([np.stack([_tile(np.asarray(w, f32), np.arange(m * 128, (m + 1) * 128)) for m in range(8)])
                          for w in wos])
    wgu = np.asarray(inp["w_gate_up"], f32)
    com["wgu"] = np.stack([np.stack([_tile(wgu[l], np.arange(f * 128, (f + 1) * 128)) for f in range(44)])
                           for l in range(4)])
    wd = np.asarray(inp["w_down"], f32)
    com["wd"] = np.ascontiguousarray(
        wd.reshape(4, NF, 128, 8, 128).transpose(0, 3, 2, 1, 4).reshape(4, 8, 128, NF * 128))
    bo = np.asarray(inp["a_b_o"], f32).reshape(2, 8, 128)
    com["bo"] = np.ascontiguousarray(bo.transpose(2, 0, 1).reshape(128, 16))
    com["sinks"] = np.asarray(inp["a_sinks"], f32).reshape(1, 32)
    return com


_NC_CACHE = {}


def kernel(**inputs):
    x = np.asarray(inputs["x"], np.float32)
    pos = np.asarray(inputs["positions"], np.int32)
    com = _prep_common(inputs)
    if "nc" not in _NC_CACHE:
        _NC_CACHE["nc"] = build()
    nc = _NC_CACHE["nc"]
    in_maps = []
    for b in range(8):
        m = dict(com)
        m["xT"] = np.ascontiguousarray(x[b].T)
        m["pos"] = np.ascontiguousarray(pos[b].reshape(1, SEQ))
        in_maps.append(m)
    res = run_bass_kernel_spmd(nc, in_maps, core_ids=list(range(8)))
    out = np.stack([np.ascontiguousarray(res.results[b]["outT"].T) for b in range(8)])
    return out.astype(np.float32)
```

```python
import contextlib
import numpy as np
import concourse.bass as bass
import concourse.mybir as mybir
from concourse.bass_utils import run_bass_kernel_spmd

F32 = mybir.dt.float32
BF16 = mybir.dt.bfloat16
I32 = mybir.dt.int32
ALU = mybir.AluOpType
AF = mybir.ActivationFunctionType
AX = mybir.AxisListType

D = 1024
SEQ = 4096
DEPTH = 4
DFF = 2816
NF = 22
ALPHA = float((2 * DEPTH) ** 0.25)
BETA = (8 * DEPTH) ** -0.25
EPS = 1e-5
KINDS = [0, 1, 2, 0]
NTL = [28, 84, 28, 28]
TWO_PI = float(2 * np.pi)
CW1 = 6.28125
CW2 = float(2 * np.pi - 6.28125)


class Buf:
    __slots__ = ("name", "w", "r")

    def __init__(self, name=""):
        self.name = name
        self.w = None
        self.r = []


class Sched:
    ENG = ("pe", "act", "dve", "pool", "sp")
    SEM_ROLL = 30000
    NDMA = 16

    def __init__(self, nc, stack):
        self.nc = nc
        self.stack = stack
        self._k = 0
        self.ops = {e: [] for e in self.ENG}
        self.cur_sem = {}
        self.cnt = {}
        for e in self.ENG:
            self._new_sem(e)
        self.dma_sems = {}
        self.dma_idx = {}
        self.waited = {e: {} for e in self.ENG}

    def _alloc(self, name):
        return self.stack.enter_context(self.nc.semaphore(name))

    def _new_sem(self, e):
        self._k += 1
        self.cur_sem[e] = self._alloc(f"s_{e}_{self._k}")
        self.cnt[e] = 0

    def _filter(self, eng, toks):
        need = {}
        wd = self.waited[eng]
        for (s, v) in toks:
            if wd.get(id(s), (None, -1))[1] >= v:
                continue
            if id(s) not in need or need[id(s)][1] < v:
                need[id(s)] = (s, v)
        for k, (s, v) in need.items():
            wd[k] = (s, v)
        return list(need.values())

    def _deps(self, eng, reads, writes):
        toks = []
        for b in reads:
            if b.w is not None:
                toks.append(b.w)
        for b in writes:
            if b.w is not None:
                toks.append(b.w)
            toks.extend(b.r)
        return self._filter(eng, toks)

    def _commit(self, tok, reads, writes):
        for b in reads:
            b.r.append(tok)
            if len(b.r) > 64:
                b.r = b.r[-64:]
        for b in writes:
            b.w = tok
            b.r = []

    def op(self, eng, fn, reads=(), writes=()):
        waits = self._deps(eng, reads, writes)
        if self.cnt[eng] >= self.SEM_ROLL:
            self._new_sem(eng)
        sem = self.cur_sem[eng]
        self.cnt[eng] += 1
        tok = (sem, self.cnt[eng])
        self.ops[eng].append((waits, fn, sem, 1))
        self._commit(tok, reads, writes)
        return tok

    def chain(self, eng, fns, reads=(), writes=()):
        tok = None
        for fn in fns:
            tok = self.op(eng, fn, reads, writes)
        return tok

    def dma(self, q, fn, reads=(), writes=()):
        if q not in self.dma_sems:
            self.dma_sems[q] = [[self._alloc(f"d_{q}_{i}"), 0] for i in range(self.NDMA)]
            self.dma_idx[q] = 0
        waits = self._deps(q, reads, writes)
        i = self.dma_idx[q]
        self.dma_idx[q] = (i + 1) % self.NDMA
        ent = self.dma_sems[q][i]
        if ent[1] + 16 > 60000:
            self._k += 1
            ent[0] = self._alloc(f"d_{q}_{i}_r{self._k}")
            ent[1] = 0
        sem, val = ent
        if val > 0:
            waits = waits + self._filter(q, [(sem, val)])
        ent[1] = val + 16
        tok = (sem, val + 16)
        self.ops[q].append((waits, fn, sem, 16))
        self._commit(tok, reads, writes)
        return tok

    def all_tokens(self):
        toks = [(self.cur_sem[e], self.cnt[e]) for e in self.ENG if self.cnt[e] > 0]
        for q in self.dma_sems:
            toks += [(ent[0], ent[1]) for ent in self.dma_sems[q] if ent[1] > 0]
        return toks

    def barrier(self):
        toks = self.all_tokens()
        for e in self.ENG:
            w = self._filter(e, toks)
            if w:
                self.ops[e].append((w, None, None, 0))

    def emit(self, block):
        sch = self

        def mk(e):
            def body(engine):
                for (waits, fn, sem, inc) in sch.ops[e]:
                    for (s, v) in waits:
                        engine.wait_ge(s, v)
                    if fn is not None:
                        ins = fn()
                        ins.then_inc(sem, inc)
            return body
        block.tensor(mk("pe"))
        block.scalar(mk("act"))
        block.vector(mk("dve"))
        block.gpsimd(mk("pool"))
        block.sync(mk("sp"))


def build(n_layers=DEPTH, stop_after_att=False):
    nc = bass.Bass("TRN2", target_bir_lowering=False)
    ein = lambda n, s, d: nc.dram_tensor(n, s, d, kind="ExternalInput").ap()
    xT = ein("xT", [D, SEQ], F32)
    pos = ein("pos", [1, SEQ], I32)
    invf = ein("invf", [128, 1], F32)
    lng_d = ein("lng", [128, 64], F32)
    lnb_d = ein("lnb", [128, 64], F32)
    watt = [ein(f"watt{l}", [NTL[l], 128, 1024], F32) for l in range(DEPTH)]
    wo_d = ein("wo", [DEPTH, 8, 128, 1024], F32)
    wgu_d = ein("wgu", [DEPTH, 44, 128, 1024], F32)
    wd_d = ein("wd", [DEPTH, 8, 128, NF * 128], F32)
    batt_d = ein("batt", [128, 56], F32)
    bv_d = ein("bv", [2, 8, 128], F32)
    bo_d = ein("bo", [128, 16], F32)
    sinks_d = ein("sinks", [1, 32], F32)
    outT = nc.dram_tensor("outT", [D, SEQ], F32, kind="ExternalOutput").ap()
    Xs = nc.dram_tensor("Xs", [D, SEQ], F32).ap()
    XB = nc.dram_tensor("XBs", [D, SEQ], BF16).ap()
    OT = nc.dram_tensor("OTs", [D, SEQ], BF16).ap()
    ROPE = nc.dram_tensor("ROPEs", [12, 128, SEQ], BF16).ap()
    TAB = nc.dram_tensor("TABs", [2, 128, SEQ], F32).ap()

    st = contextlib.ExitStack()
    with st:
        ARENA_B = 212800
        arena = st.enter_context(nc.sbuf_tensor("arena", [128, ARENA_B // 2], BF16))
        PS = [st.enter_context(nc.psum_tensor(f"ps{i}", [128, 512], F32)) for i in range(8)]
        PB = [Buf(f"ps{i}") for i in range(8)]
        blk = st.enter_context(nc.Block())
        S = Sched(nc, st)

        class Al:
            off = 0

        def take(nbytes):
            o = Al.off
            Al.off += (nbytes + 63) // 64 * 64
            assert Al.off <= ARENA_B, Al.off
            return o

        def V(off, nbytes, dt=BF16):
            v = arena[:, off // 2:(off + nbytes) // 2]
            if dt != BF16:
                v = v.bitcast(dt)
            return v

        o_mask = take(2 * 1024)
        maskA = V(o_mask, 1024)
        maskB = V(o_mask + 1024, 1024)
        ones32 = V(take(256), 256, F32)
        onesb = V(take(256), 256)
        ident = V(take(256), 256)
        lng = V(take(256), 256, F32)
        lnb = V(take(256), 256, F32)
        batt = V(take(224), 224, F32)
        bo = V(take(64), 64, F32)
        esk = V(take(128), 128, F32)
        invf_t = V(take(64), 4, F32)
        bvt = V(take(512), 512, F32)
        B_const = Buf("const")
        B_bvt = Buf("bvt")
        base_off = Al.off

        Al.off = base_off
        o_xb = take(64 * 1024)
        xb = V(o_xb, 64 * 1024).rearrange("p (c t) -> p c t", c=8)
        B_xb = Buf("xb")
        wsl = []
        for i in range(2):
            o = take(6 * 1024)
            wsl.append((V(o, 6 * 1024).rearrange("p (t c j) -> p t c j", t=3, c=8), Buf(f"wsl{i}")))
        QKV = []

        def mk_qkv(o):
            q = V(o, 8192)
            ka = V(o + 8192, 8192)
            kb = V(o + 16384, 8192)
            va = V(o + 24576, 8320).rearrange("p (t h e) -> p t h e", t=32, h=2)
            return (q, ka, kb, va, Buf("QT"), Buf("K"), Buf("V"))
        QKV.append(mk_qkv(take(24576 + 8320)))
        o_oacc = take(32 * 1024)
        oacc = V(o_oacc, 32 * 1024, F32).rearrange("p (h t) -> p h t", h=2)
        B_oacc = Buf("oacc")
        COS = V(o_oacc, 16 * 1024, F32)
        SIN = V(o_oacc + 16 * 1024, 16 * 1024, F32)
        o_onb = take(16 * 1024)
        onb = V(o_onb, 16 * 1024).rearrange("p (h t) -> p h t", h=2)
        B_onb = Buf("onb")
        RotF = [V(o_onb, 8192), V(o_onb + 8192, 8192)]
        B_rot = [Buf("rot0"), Buf("rot1")]
        Pt = []
        for i in range(4):
            Pt.append((V(take(1024), 1024), Buf(f"pt{i}")))
        nrm = [(V(take(2048), 2048, F32), Buf(f"nrm{i}")) for i in range(2)]
        o_q1 = take(24576 + 8320)
        QKV.append(mk_qkv(o_q1))
        Rf = []
        for i in range(4):
            Rf.append((V(o_q1 + i * 2048, 2048, F32), Buf(f"rf{i}")))
        tt_ = [(V(o_q1 + 8192 + i * 2048, 2048, F32), Buf(f"tt{i}")) for i in range(2)]
        att_end = Al.off

        accC = V(o_oacc, 16640, F32).rearrange("p (t h e) -> p t h e", t=32, h=2)
        SEL = V(o_oacc + 16640, 4096, F32).rearrange("p (t h n) -> p t h n", t=32, h=2)
        Gt = V(o_oacc + 20736, 128, F32).rearrange("p (h n) -> p h n", h=2)
        top8 = V(o_oacc + 20864, 64, F32).rearrange("p (h n) -> p h n", h=2)
        kmf = V(o_oacc + 20928, 128, F32).rearrange("p (h n) -> p h n", h=2)
        kmh = V(o_oacc + 21056, 64).rearrange("p (h n) -> p h n", h=2)
        kml = V(o_oacc + 21120, 64).rearrange("p (h n) -> p h n", h=2)
        kmr = V(o_oacc + 21184, 128, F32).rearrange("p (h n) -> p h n", h=2)
        rcpC = V(o_oacc + 21312, 256, F32).rearrange("p (t h) -> p t h", t=32)
        obC = V(o_onb, 8192).rearrange("p (t e) -> p t e", t=32)
        onT = V(o_onb + 8192, 8192)
        B_sel, B_g, B_km, B_obC, B_onT = Buf("sel"), Buf("g"), Buf("km"), Buf("obC"), Buf("onT")

        Al.off = base_off
        Wgu = V(take(88 * 1024), 88 * 1024).rearrange("p (f n) -> p f n", f=44)
        B_wgu = [Buf(f"wgu{i}") for i in range(11)]
        wds = [(V(take(5632), 5632).rearrange("p (f j) -> p f j", f=NF), Buf(f"wds{i}")) for i in range(4)]
        wos = []
        for i in range(2):
            wos.append((V(take(2048), 2048), Buf(f"wos{i}")))
        TP = 512
        ZT = [V(take(16384), 16384, F32).rearrange("p (c t) -> p c t", c=8) for i in range(2)]
        BZ = [[Buf(f"z{i}_{m}") for m in range(8)] for i in range(2)]
        oTc = V(take(8192), 8192).rearrange("p (c t) -> p c t", c=8)
        B_oTc = Buf("oTc")
        x1b = V(take(8192), 8192).rearrange("p (c t) -> p c t", c=8)
        B_x1b = Buf("x1b")
        aT = V(take(22 * 1024), 22 * 1024).rearrange("p (f t) -> p f t", f=NF)
        B_aT = [Buf(f"aT{f}") for f in range(NF)]
        mean_sb = V(take(2048), 2048, F32)
        rstd_sb = V(take(2048), 2048, F32)
        B_stat = Buf("stat")
        zb = [(V(take(1024), 1024), Buf(f"zb{i}")) for i in range(4)]
        zsq = [(V(take(1024), 1024), Buf(f"zsq{i}")) for i in range(4)]
        lt = [(V(take(2048), 2048, F32), Buf(f"lt{i}")) for i in range(2)]
        sg = [(V(take(1024), 1024), Buf(f"sg{i}")) for i in range(2)]
        post_end = Al.off
        assert max(att_end, post_end) <= ARENA_B

        sp, act, dve, pool, pe = nc.sync, nc.scalar, nc.vector, nc.gpsimd, nc.tensor

        mA = maskA.rearrange("p (s i) -> p s i", s=4)
        mB = maskB.rearrange("p (s i) -> p s i", s=4)
        cfns = [lambda: pool.memset(arena[:, o_mask // 2:(o_mask + 2048) // 2], 1.0)]
        for s_ in range(2):
            cfns.append(lambda s_=s_: pool.affine_select(out=mA[:, s_, :], in_=mA[:, s_, :], pattern=[[-1, 128]],
                        compare_op=ALU.is_ge, fill=0.0, base=-1, channel_multiplier=1))
            cfns.append(lambda s_=s_: pool.affine_select(out=mB[:, s_, :], in_=mB[:, s_, :], pattern=[[-1, 128]],
                        compare_op=ALU.is_ge, fill=0.0, base=0, channel_multiplier=1))
        for s_ in range(2, 4):
            for mm_ in (mA, mB):
                cfns.append(lambda s_=s_, mm_=mm_: pool.affine_select(out=mm_[:, s_, :], in_=mm_[:, s_, :],
                            pattern=[[1, 128]], compare_op=ALU.is_ge, fill=0.0, base=0, channel_multiplier=-1))
        cfns.append(lambda: pool.memset(ones32, 1.0))
        cfns.append(lambda: pool.memset(onesb, 1.0 / 1024.0))
        cfns.append(lambda: pool.memset(ident, 1.0))
        cfns.append(lambda: pool.affine_select(out=ident, in_=ident, pattern=[[1, 128]],
                                               compare_op=ALU.is_equal, fill=0.0, base=0, channel_multiplier=-1))
        S.chain("pool", cfns, writes=[B_const])
        B_small = Buf("small")
        for dst, src in ((lng, lng_d), (lnb, lnb_d), (batt, batt_d), (bo, bo_d), (invf_t, invf)):
            S.dma("sp", lambda dst=dst, src=src: sp.dma_start(out=dst, in_=src), writes=[B_small])
        S.dma("sp", lambda: sp.dma_start(out=esk, in_=sinks_d.broadcast_to([128, 32])), writes=[B_small])
        S.op("act", lambda: act.activation(out=esk, in_=esk, func=AF.Exp), reads=[B_small], writes=[B_small])

        posi = V(o_xb, 16384, I32)
        ang = V(o_xb + 16384, 16384, F32)
        kf = V(o_xb + 32768, 16384, F32)
        ki = V(o_xb + 49152, 16384, I32)
        B_tab = Buf("tab")
        S.dma("sp", lambda: sp.dma_start(out=posi, in_=pos.broadcast_to([128, SEQ])), writes=[B_tab])

        S.chain("dve", [
            lambda: dve.tensor_copy(out=ang, in_=posi),
            lambda: dve.tensor_scalar(out=ang, in0=ang, scalar1=invf_t[:, 0:1], scalar2=None, op0=ALU.mult),
            lambda: dve.tensor_scalar(out=kf, in0=ang, scalar1=float(1.0 / TWO_PI), scalar2=None, op0=ALU.mult),
            lambda: dve.tensor_copy(out=ki, in_=kf),
            lambda: dve.tensor_copy(out=kf, in_=ki),
            lambda: dve.scalar_tensor_tensor(out=ang, in0=kf, scalar=-CW1, in1=ang, op0=ALU.mult, op1=ALU.add),
            lambda: dve.scalar_tensor_tensor(out=ang, in0=kf, scalar=-CW2, in1=ang, op0=ALU.mult, op1=ALU.add),
            lambda: dve.tensor_scalar(out=ang, in0=ang, scalar1=float(np.pi), scalar2=float(-np.pi),
                                      op0=ALU.min, op1=ALU.max),
            lambda: dve.scalar_tensor_tensor(out=kf, in0=ang, scalar=-1.0, in1=ang, op0=ALU.mult, op1=ALU.max),
            lambda: dve.tensor_scalar(out=kf, in0=kf, scalar1=-1.0, scalar2=float(np.pi / 2),
                                      op0=ALU.mult, op1=ALU.add),
        ], reads=[B_small], writes=[B_tab])
        S.op("act", lambda: act.activation(out=ang, in_=ang, func=AF.Sin), writes=[B_tab])
        S.op("act", lambda: act.activation(out=kf, in_=kf, func=AF.Sin), writes=[B_tab])
        S.dma("sp", lambda: sp.dma_start(out=TAB[1], in_=ang), reads=[B_tab])
        S.dma("sp", lambda: sp.dma_start(out=TAB[0], in_=kf), reads=[B_tab])
        S.barrier()
        xTv = xT.rearrange("(c p) t -> p c t", p=128)
        for c in range(8):
            for h_ in range(2):
                S.dma("pool", lambda c=c, h_=h_: pool.dma_start(
                    out=xb[:, c, h_ * 2048:(h_ + 1) * 2048], in_=xTv[:, c, h_ * 2048:(h_ + 1) * 2048]),
                    writes=[B_xb])

        pcnt = [0]

        def proj_bank():
            pcnt[0] += 1
            return pcnt[0] % 2

        def load_w(l, t0, nt):
            slot, bslot = wsl[load_w.i % 2]
            load_w.i += 1
            src = watt[l][t0:t0 + nt].rearrange("t p n -> p t n")
            dstv = slot[:, 0:nt].rearrange("p t c j -> p t (c j)")
            S.dma("pool", lambda: pool.dma_start(out=dstv, in_=src), writes=[bslot])
            return slot, bslot
        load_w.i = 0

        def phase_view(ap2d, d):
            return ap2d.rearrange("p (j r) -> p r j", r=d)

        def att_common_start(l):
            if l > 0:
                Xv_ = Xs.rearrange("(c p) t -> p c t", p=128)
                for c in range(8):
                    for h_ in range(2):
                        S.dma("pool", lambda c=c, h_=h_: pool.dma_start(
                            out=xb[:, c, h_ * 2048:(h_ + 1) * 2048], in_=Xv_[:, c, h_ * 2048:(h_ + 1) * 2048]),
                            writes=[B_xb])
            S.dma("sp", lambda: sp.dma_start(out=COS, in_=TAB[0]), writes=[B_oacc])
            S.dma("sp", lambda: sp.dma_start(out=SIN, in_=TAB[1]), writes=[B_oacc])

        def rope_prepass(l, groups, la):
            for g, (d, t_base) in enumerate(groups):
                for qk in range(2):
                    t0 = t_base + 2 * qk
                    slot, bslot = load_w(l, t0, 2)
                    L = SEQ // d
                    for tc in range(8):
                        rfs = []
                        for half in range(2):
                            b = proj_bank()
                            rf, brf = Rf[(tc % 2) * 2 + half]

                            def mmf(b=b, half=half, tc=tc, slot=slot):
                                for c in range(8):
                                    ins = pe.matmul(PS[b][:, :], lhsT=slot[:, half, c, :],
                                                    rhs=xb[:, c, tc * 512:(tc + 1) * 512],
                                                    start=(c == 0), stop=(c == 7))
                                return ins
                            S.op("pe", mmf, reads=[bslot, B_xb], writes=[PB[b]])
                            if la is not None:
                                bias_ap = batt[:, la * 28 + t0 + half:la * 28 + t0 + half + 1]
                                S.op("act", lambda b=b, rf=rf, bias_ap=bias_ap: act.activation(
                                    out=rf, in_=PS[b][:, :], func=AF.Identity, bias=bias_ap),
                                    reads=[PB[b], B_small], writes=[brf])
                            else:
                                S.op("act", lambda b=b, rf=rf: act.copy(out=rf, in_=PS[b][:, :]),
                                     reads=[PB[b]], writes=[brf])
                            rfs.append((rf, brf))
                        (r1, b1), (r2, b2) = rfs
                        (ta, bta), (tb, btb) = tt_
                        cs = COS[:, tc * 512:(tc + 1) * 512]
                        sn = SIN[:, tc * 512:(tc + 1) * 512]
                        w = 512 // d
                        o1 = RotF[0].rearrange("p (r j) -> p r j", r=d)[:, :, tc * w:(tc + 1) * w]
                        o2 = RotF[1].rearrange("p (r j) -> p r j", r=d)[:, :, tc * w:(tc + 1) * w]

                        pv_ = lambda a, d=d: phase_view(a, d)
                        S.chain("dve", [
                            lambda r1=r1, ta=ta, cs=cs: dve.tensor_tensor(out=ta, in0=r1, in1=cs, op=ALU.mult),
                            lambda r2=r2, tb=tb, sn=sn: dve.tensor_tensor(out=tb, in0=r2, in1=sn, op=ALU.mult),
                            lambda o1=o1, ta=ta, tb=tb, pv_=pv_: dve.tensor_tensor(out=o1, in0=pv_(ta), in1=pv_(tb),
                                                                                  op=ALU.subtract),
                            lambda r2=r2, ta=ta, cs=cs: dve.tensor_tensor(out=ta, in0=r2, in1=cs, op=ALU.mult),
                            lambda r1=r1, tb=tb, sn=sn: dve.tensor_tensor(out=tb, in0=r1, in1=sn, op=ALU.mult),
                            lambda o2=o2, ta=ta, tb=tb, pv_=pv_: dve.tensor_tensor(out=o2, in0=pv_(ta), in1=pv_(tb),
                                                                                  op=ALU.add),
                        ], reads=[b1, b2, B_oacc], writes=[bta, btb, B_rot[0], B_rot[1]])
                    for half in range(2):
                        idx = (g * 2 + qk) * 2 + half
                        S.dma("sp", lambda idx=idx, half=half: sp.dma_start(out=ROPE[idx], in_=RotF[half]),
                              reads=[B_rot[half]])

        def project_pair(l, p, g, d, t_base, la, bs):
            QT, KA, KB, Vaug, B_Q, B_K, B_V = QKV[bs]
            KK = [KA, KB]
            t0 = t_base + 4 + 3 * p
            slot, bslot = load_w(l, t0, 3)
            w = 512 // d
            QTv = QT.rearrange("p (r j) -> p r j", r=d)
            KAv = KA.rearrange("p (r j) -> p r j", r=d)
            KBv = KB.rearrange("p (r j) -> p r j", r=d)
            for tc in range(8):
                b = proj_bank()

                def mmq(b=b, tc=tc):
                    for c in range(8):
                        ins = pe.matmul(PS[b][:, :], lhsT=slot[:, 0, c, :], rhs=xb[:, c, tc * 512:(tc + 1) * 512],
                                        start=(c == 0), stop=(c == 7))
                    return ins
                S.op("pe", mmq, reads=[bslot, B_xb], writes=[PB[b]])
                qo = QTv[:, :, tc * w:(tc + 1) * w]
                if la is not None:
                    bq = batt[:, la * 28 + t0:la * 28 + t0 + 1]
                    S.op("act", lambda b=b, qo=qo, bq=bq: act.activation(
                        out=qo, in_=phase_view(PS[b][:, :], d), func=AF.Identity, bias=bq),
                        reads=[PB[b], B_small], writes=[B_Q])
                else:
                    S.op("act", lambda b=b, qo=qo: act.copy(out=qo, in_=phase_view(PS[b][:, :], d)),
                         reads=[PB[b]], writes=[B_Q])
                b = proj_bank()

                def mmk(b=b, tc=tc):
                    for c in range(8):
                        ins = pe.matmul(PS[b][:, :], lhsT=slot[:, 1, c, :], rhs=xb[:, c, tc * 512:(tc + 1) * 512],
                                        start=(c == 0), stop=(c == 7))
                    return ins
                S.op("pe", mmk, reads=[bslot, B_xb], writes=[PB[b]])
                ka_o = KAv[0:64, :, tc * w:(tc + 1) * w]
                kb_o = KBv[64:128, :, tc * w:(tc + 1) * w]
                if la is not None:
                    bk = batt[:, la * 28 + t0 + 1:la * 28 + t0 + 2]
                    S.op("act", lambda b=b, ka_o=ka_o, bk=bk: act.activation(
                        out=ka_o, in_=phase_view(PS[b][0:64, :], d), func=AF.Identity, bias=bk[0:64, :]),
                        reads=[PB[b], B_small], writes=[B_K])
                    S.op("dve", lambda b=b, kb_o=kb_o, bk=bk: dve.tensor_scalar(
                        out=kb_o, in0=phase_view(PS[b][64:128, :], d), scalar1=bk[64:128, :], scalar2=None,
                        op0=ALU.add), reads=[PB[b], B_small], writes=[B_K])
                else:
                    S.op("act", lambda b=b, ka_o=ka_o: act.copy(out=ka_o, in_=phase_view(PS[b][0:64, :], d)),
                         reads=[PB[b]], writes=[B_K])
                    S.op("dve", lambda b=b, kb_o=kb_o: dve.tensor_copy(out=kb_o, in_=phase_view(PS[b][64:128, :], d)),
                         reads=[PB[b]], writes=[B_K])
                yield
            nbp = 32 // d
            if la is not None:
                S.dma("sp", lambda: sp.dma_start(out=bvt, in_=bv_d[la, p:p + 1, :].broadcast_to([128, 128])),
                      writes=[B_bvt])
            for t4 in range(8):
                b = proj_bank()

                def mmv(b=b, t4=t4):
                    for j in range(4):
                        tt = t4 * 4 + j
                        r, n = divmod(tt, nbp)
                        s0 = r + d * 128 * n
                        for c in range(8):
                            ins = pe.matmul(PS[b][:, j * 128:(j + 1) * 128],
                                            lhsT=xb[:, c, s0:s0 + d * 127 + 1:d], rhs=slot[:, 2, c, :],
                                            start=(c == 0), stop=(c == 7))
                    return ins
                S.op("pe", mmv, reads=[bslot, B_xb], writes=[PB[b]])
                vo = Vaug[:, t4 * 4:(t4 + 1) * 4, :, 0:64]
                pv = PS[b][:, :].rearrange("p (t h e) -> p t h e", t=4, h=2)
                if la is not None:
                    bvv = bvt.rearrange("p (h e) -> p h e", h=2).unsqueeze(1).broadcast_to([128, 4, 2, 64])
                    S.op("dve", lambda vo=vo, pv=pv, bvv=bvv: dve.tensor_tensor(out=vo, in0=pv, in1=bvv, op=ALU.add),
                         reads=[PB[b], B_bvt], writes=[B_V])
                else:
                    S.op("dve", lambda vo=vo, pv=pv: dve.tensor_copy(out=vo, in_=pv), reads=[PB[b]], writes=[B_V])
                yield
            for hh in range(2):
                h = 2 * p + hh
                for half in range(2):
                    iq = (g * 2 + 0) * 2 + half
                    ik = (g * 2 + 1) * 2 + half
                    r0 = hh * 64 + half * 8
                    S.dma("sp", lambda iq=iq, r0=r0, h=h: sp.dma_start(
                        out=QT[r0:r0 + 8, :], in_=ROPE[iq][h * 8:(h + 1) * 8, :]), writes=[B_Q])
                    S.dma("sp", lambda ik=ik, r0=r0, h=h, hh=hh: sp.dma_start(
                        out=KK[hh][r0:r0 + 8, :], in_=ROPE[ik][h * 8:(h + 1) * 8, :]), writes=[B_K])
            yield

        def att_init_kv():
            for bs in range(2):
                QT, KA, KB, Vaug, B_Q, B_K, B_V = QKV[bs]

                def f(KA=KA, KB=KB, Vaug=Vaug):
                    pool.memset(KA[64:128, :], 0.0)
                    pool.memset(KB[0:64, :], 0.0)
                    return pool.memset(Vaug[:, :, :, 64:65], 1.0)
                S.op("pool", f, writes=[B_K, B_V])

        def banded_attention(d, mask, first, bs):
            QT, KA, KB, Vaug, B_Q, B_K, B_V = QKV[bs]
            KK = [KA, KB]
            nbp = 32 // d

            SBK = [2, 3, 6, 7]

            def s_stage(qt):
                r, n = divmod(qt, nbp)
                hp = n > 0
                sb = SBK[qt % 4]

                def f(qt=qt, hp=hp, sb=sb):
                    q = QT[:, qt * 128:(qt + 1) * 128]
                    ins = None
                    for hh in range(2):
                        if hp:
                            pe.matmul(PS[sb][:, hh * 128:(hh + 1) * 128], lhsT=KK[hh][:, (qt - 1) * 128:qt * 128],
                                      rhs=q, start=True, stop=True)
                        ins = pe.matmul(PS[sb][:, (2 + hh) * 128:(3 + hh) * 128],
                                        lhsT=KK[hh][:, qt * 128:(qt + 1) * 128], rhs=q, start=True, stop=True)
                    return ins
                S.op("pe", f, reads=[B_Q, B_K], writes=[PB[sb]])

            def rest(qt):
                r, n = divmod(qt, nbp)
                hp = n > 0
                sb = SBK[qt % 4]
                ob = 4 + qt % 2
                pt, bpt = Pt[qt % 4]
                lo = 0 if hp else 256
                S.op("act", lambda: act.activation(out=pt[:, lo:512], in_=PS[sb][:, lo:512], func=AF.Exp,
                                                   scale=0.125), reads=[PB[sb]], writes=[bpt])
                S.op("pool", lambda: pool.tensor_tensor(out=pt[:, lo:512], in0=pt[:, lo:512], in1=mask[:, lo:512],
                                                        op=ALU.mult), reads=[B_const], writes=[bpt])

                def pv():
                    ins = None
                    for hh in range(2):
                        o = PS[ob][0:65, hh * 128:(hh + 1) * 128]
                        if hp:
                            pe.matmul(o, lhsT=Vaug[:, qt - 1, hh, :], rhs=pt[:, hh * 128:(hh + 1) * 128],
                                      start=True, stop=False)
                        ins = pe.matmul(o, lhsT=Vaug[:, qt, hh, :], rhs=pt[:, (2 + hh) * 128:(3 + hh) * 128],
                                        start=(not hp), stop=True)
                    return ins
                S.op("pe", pv, reads=[bpt, B_V], writes=[PB[ob]])
                s0 = r + d * 128 * n
                dst = oacc[0:65, :, s0:s0 + d * 127 + 1:d]
                src = PS[ob][0:65, 0:256].rearrange("p (h q) -> p h q", h=2)
                if first:
                    S.op("dve", lambda: dve.tensor_copy(out=dst, in_=src), reads=[PB[ob]], writes=[B_oacc])
                else:
                    S.op("dve", lambda: dve.tensor_tensor(out=dst, in0=dst, in1=src, op=ALU.add),
                         reads=[PB[ob]], writes=[B_oacc])
            s_stage(0)
            s_stage(1)
            s_stage(2)
            for qt in range(32):
                if qt + 3 < 32:
                    s_stage(qt + 3)
                rest(qt)
                yield

        def normalize_store(p, la):
            k = 0
            for hh in range(2):
                h = 2 * p + hh
                for tc in range(8):
                    b = 6 + k % 2
                    nb_, bnb = nrm[k % 2]
                    k += 1
                    S.op("pe", lambda b=b, hh=hh, tc=tc: pe.matmul(
                        PS[b][0:64, :], lhsT=ones32[64:65, 0:64], rhs=oacc[64:65, hh, tc * 512:(tc + 1) * 512],
                        start=True, stop=True), reads=[B_oacc, B_const], writes=[PB[b]])
                    if la is not None:
                        sk = esk[0:64, la * 16 + h:la * 16 + h + 1]
                        S.op("act", lambda b=b, nb_=nb_, sk=sk: act.activation(
                            out=nb_[0:64, :], in_=PS[b][0:64, :], func=AF.Ln, bias=sk),
                            reads=[PB[b], B_small], writes=[bnb])
                    else:
                        S.op("act", lambda b=b, nb_=nb_: act.activation(
                            out=nb_[0:64, :], in_=PS[b][0:64, :], func=AF.Ln), reads=[PB[b]], writes=[bnb])
                    S.op("act", lambda nb_=nb_: act.activation(out=nb_[0:64, :], in_=nb_[0:64, :], func=AF.Exp,
                                                               scale=-1.0), writes=[bnb])
                    S.op("dve", lambda hh=hh, tc=tc, nb_=nb_: dve.tensor_tensor(
                        out=onb[0:64, hh, tc * 512:(tc + 1) * 512], in0=oacc[0:64, hh, tc * 512:(tc + 1) * 512],
                        in1=nb_[0:64, :], op=ALU.mult), reads=[bnb, B_oacc], writes=[B_onb])
            for hh in range(2):
                h = 2 * p + hh
                S.dma("sp", lambda hh=hh, h=h: sp.dma_start(out=OT[h * 64:(h + 1) * 64, :], in_=onb[0:64, hh, :]),
                      reads=[B_onb])

        MOBA_SB = [2, 3, 6, 7]

        def moba_pair(p, bs):
            QT, KA, KB, Vaug, B_Q, B_K, B_V = QKV[bs]
            KK = [KA, KB]
            S.op("pool", lambda: pool.memset(accC, 0.0), writes=[B_oacc])
            S.chain("dve", [
                lambda: dve.tensor_reduce(out=kmf[:, 0, :], in_=KK[0].rearrange("p (n t) -> p n t", n=16),
                                          axis=AX.X, op=ALU.add),
                lambda: dve.tensor_reduce(out=kmf[:, 1, :], in_=KK[1].rearrange("p (n t) -> p n t", n=16),
                                          axis=AX.X, op=ALU.add),
                lambda: dve.tensor_scalar(out=kmf, in0=kmf, scalar1=1.0 / 256.0, scalar2=None, op0=ALU.mult),
                lambda: dve.tensor_copy(out=kmh, in_=kmf),
                lambda: dve.tensor_copy(out=kmr, in_=kmh),
                lambda: dve.tensor_tensor(out=kmr, in0=kmf, in1=kmr, op=ALU.subtract),
                lambda: dve.tensor_copy(out=kml, in_=kmr),
            ], reads=[B_K], writes=[B_km])
            gb = 6
            for qt in range(2, 32):
                npast = qt // 2

                def gmm(qt=qt):
                    q = QT[:, qt * 128:(qt + 1) * 128]
                    ins = None
                    for hh in range(2):
                        pe.matmul(PS[gb][:, hh * 16:(hh + 1) * 16], lhsT=q, rhs=kmh[:, hh, :], start=True, stop=False)
                        ins = pe.matmul(PS[gb][:, hh * 16:(hh + 1) * 16], lhsT=q, rhs=kml[:, hh, :],
                                        start=False, stop=True)
                    return ins
                S.op("pe", gmm, reads=[B_Q, B_km], writes=[PB[gb]])
                S.chain("dve", [
                    lambda: dve.memset(Gt, -1e30),
                    lambda npast=npast: dve.tensor_copy(
                        out=Gt[:, :, 0:npast],
                        in_=PS[gb][:, 0:32].rearrange("p (h n) -> p h n", h=2)[:, :, 0:npast]),
                    lambda: dve.max(out=top8[:, 0, :], in_=Gt[:, 0, :]),
                    lambda: dve.max(out=top8[:, 1, :], in_=Gt[:, 1, :]),
                    lambda qt=qt: dve.tensor_scalar(out=SEL[:, qt, 0, :], in0=Gt[:, 0, :], scalar1=top8[:, 0, 2:3],
                                                    scalar2=None, op0=ALU.is_ge),
                    lambda qt=qt: dve.tensor_scalar(out=SEL[:, qt, 1, :], in0=Gt[:, 1, :], scalar1=top8[:, 1, 2:3],
                                                    scalar2=None, op0=ALU.is_ge),
                ], reads=[PB[gb]], writes=[B_sel, B_g])
                if qt % 4 == 3:
                    yield

            cnt = [0]
            ocnt = [0]

            def s_exp(hh, kt, q0, nq, tri):
                i = cnt[0]
                cnt[0] += 1
                sb = MOBA_SB[i % 4]
                pt, bpt = Pt[i % 4]
                K = KK[hh]
                S.op("pe", lambda: pe.matmul(PS[sb][:, 0:nq], lhsT=K[:, kt * 128:(kt + 1) * 128],
                                             rhs=QT[:, q0:q0 + nq], start=True, stop=True),
                     reads=[B_Q, B_K], writes=[PB[sb]])
                S.op("act", lambda: act.activation(out=pt[:, 0:nq], in_=PS[sb][:, 0:nq], func=AF.Exp, scale=0.125),
                     reads=[PB[sb]], writes=[bpt])
                if tri:
                    S.op("pool", lambda: pool.tensor_tensor(out=pt[:, 0:128], in0=pt[:, 0:128], in1=maskA[:, 256:384],
                                                            op=ALU.mult), reads=[B_const], writes=[bpt])
                return pt, bpt

            def stage_a(item):
                kind, hh, qc, x = item
                if kind == "past":
                    n = x
                    return [s_exp(hh, 2 * n + j, qc * 512, 512, False) for j in range(2)]
                qt = qc * 4 + x
                if kind == "own":
                    kts = ([qt - 1] if qt % 2 == 1 else []) + [qt]
                    return [s_exp(hh, kt, qt * 128, 128, kt == qt) for kt in kts]
                n = 2 * qc
                return [s_exp(hh, 2 * n + j, qt * 128, 128, False) for j in range(2)]

            def stage_b(item, pts):
                kind, hh, qc, x = item
                ob = 4 + ocnt[0] % 2
                ocnt[0] += 1
                rd = [t[1] for t in pts] + [B_V]
                if kind == "past":
                    n = x

                    def pvf():
                        ins = None
                        for qi in range(4):
                            for j in range(2):
                                ins = pe.matmul(PS[ob][:, qi * 65:(qi + 1) * 65],
                                                lhsT=pts[j][0][:, qi * 128:(qi + 1) * 128],
                                                rhs=Vaug[:, 2 * n + j, hh, :], start=(j == 0), stop=(j == 1))
                        return ins
                    S.op("pe", pvf, reads=rd, writes=[PB[ob]])

                    def accf():
                        ins = None
                        for qi in range(4):
                            qt = qc * 4 + qi
                            a = accC[:, qt, hh, :]
                            ins = dve.scalar_tensor_tensor(out=a, in0=PS[ob][:, qi * 65:(qi + 1) * 65],
                                                           scalar=SEL[:, qt, hh, n:n + 1], in1=a,
                                                           op0=ALU.mult, op1=ALU.add)
                        return ins
                    S.op("dve", accf, reads=[PB[ob], B_sel], writes=[B_oacc])
                    return
                qt = qc * 4 + x
                a = accC[:, qt, hh, :]
                if kind == "own":
                    kts = ([qt - 1] if qt % 2 == 1 else []) + [qt]
                else:
                    n = 2 * qc
                    kts = [2 * n, 2 * n + 1]

                def pvo():
                    ins = None
                    for j, kt in enumerate(kts):
                        ins = pe.matmul(PS[ob][:, 0:65], lhsT=pts[j][0][:, 0:128], rhs=Vaug[:, kt, hh, :],
                                        start=(j == 0), stop=(j == len(kts) - 1))
                    return ins
                S.op("pe", pvo, reads=rd, writes=[PB[ob]])
                if kind == "own":
                    S.op("dve", lambda: dve.tensor_tensor(out=a, in0=a, in1=PS[ob][:, 0:65], op=ALU.add),
                         reads=[PB[ob]], writes=[B_oacc])
                else:
                    n = 2 * qc
                    S.op("dve", lambda: dve.scalar_tensor_tensor(
                        out=a, in0=PS[ob][:, 0:65], scalar=SEL[:, qt, hh, n:n + 1], in1=a,
                        op0=ALU.mult, op1=ALU.add), reads=[PB[ob], B_sel], writes=[B_oacc])

            items = []
            for hh in range(2):
                for qc in range(8):
                    for n in range(2 * qc):
                        items.append(("past", hh, qc, n))
                    for qi in range(4):
                        items.append(("own", hh, qc, qi))
                        if qi >= 2:
                            items.append(("gated", hh, qc, qi))
            cur = stage_a(items[0])
            for i, it in enumerate(items):
                nxt = stage_a(items[i + 1]) if i + 1 < len(items) else None
                stage_b(it, cur)
                cur = nxt
                if i % 2 == 1:
                    yield
            S.chain("dve", [
                lambda: dve.reciprocal(out=rcpC, in_=accC[:, :, :, 64]),
                lambda: dve.tensor_tensor(
                    out=obC.rearrange("p t (h e) -> p t h e", h=2), in0=accC[:, :, :, 0:64],
                    in1=rcpC.unsqueeze(3).broadcast_to([128, 32, 2, 64]), op=ALU.mult),
            ], reads=[B_oacc], writes=[B_obC, B_oacc])
            for t4 in range(8):
                tb = 7
                psb = PS[tb][:, :].bitcast(BF16)

                def trf(t4=t4, psb=psb):
                    ins = None
                    for j in range(4):
                        ins = pe.transpose(psb[:, j * 128:(j + 1) * 128], obC[:, t4 * 4 + j, :], ident)
                    return ins
                S.op("pe", trf, reads=[B_obC, B_const], writes=[PB[tb]])
                S.op("act", lambda t4=t4, psb=psb: act.copy(out=onT[:, t4 * 512:(t4 + 1) * 512], in_=psb[:, 0:512]),
                     reads=[PB[tb]], writes=[B_onT])
            S.dma("sp", lambda: sp.dma_start(out=OT[p * 128:(p + 1) * 128, :], in_=onT), reads=[B_onT])
            yield

        def run_all(gen):
            for _ in gen:
                pass

        def interleave(a, b):
            gens = [a, b]
            while any(g is not None for g in gens):
                for i in range(2):
                    if gens[i] is not None:
                        try:
                            next(gens[i])
                        except StopIteration:
                            gens[i] = None

        ctr = {"zb": 0, "lt": 0, "sg": 0, "ob": 0, "gu": 0, "wo": 0, "wd": 0}

        def layer_norm(lidx, zt, B_z, want_bf16):
            mb, eb = 0, 1
            pend = []

            def pe_stat(m, zbt, bzb, zst, bzs):
                S.op("pe", lambda: pe.matmul(PS[mb][:, :], lhsT=onesb, rhs=zbt, start=(m == 0), stop=(m == 7)),
                     reads=[bzb, B_const], writes=[PB[mb]])
                S.op("pe", lambda: pe.matmul(PS[eb][:, :], lhsT=onesb, rhs=zst, start=(m == 0), stop=(m == 7)),
                     reads=[bzs, B_const], writes=[PB[eb]])
            for m in range(8):
                i = ctr["zb"] % 4
                ctr["zb"] += 1
                zbt, bzb = zb[i]
                zst, bzs = zsq[i]
                S.op("act", lambda m=m, zbt=zbt: act.copy(out=zbt, in_=zt[:, m, :]), reads=[B_z[m]], writes=[bzb])
                S.op("act", lambda m=m, zst=zst: act.activation(out=zst, in_=zt[:, m, :], func=AF.Square),
                     reads=[B_z[m]], writes=[bzs])
                pend.append((m, zbt, bzb, zst, bzs))
                if len(pend) > 2:
                    pe_stat(*pend.pop(0))
                yield
            for pnd in pend:
                pe_stat(*pnd)
            yield
            S.chain("dve", [
                lambda: dve.tensor_copy(out=mean_sb, in_=PS[mb][:, :]),
                lambda: dve.tensor_tensor(out=rstd_sb, in0=mean_sb, in1=mean_sb, op=ALU.mult),
                lambda: dve.tensor_tensor(out=rstd_sb, in0=PS[eb][:, :], in1=rstd_sb, op=ALU.subtract),
                lambda: dve.tensor_scalar(out=rstd_sb, in0=rstd_sb, scalar1=EPS, scalar2=None, op0=ALU.add),
            ], reads=[PB[mb], PB[eb]], writes=[B_stat])
            S.op("act", lambda: act.activation(out=rstd_sb, in_=rstd_sb, func=AF.Ln), writes=[B_stat])
            S.op("act", lambda: act.activation(out=rstd_sb, in_=rstd_sb, func=AF.Exp, scale=-0.5), writes=[B_stat])
            yield
            for m in range(8):
                t_, bt_ = lt[ctr["lt"] % 2]
                ctr["lt"] += 1
                S.chain("dve", [
                    lambda m=m, t_=t_: dve.tensor_tensor(out=t_, in0=zt[:, m, :], in1=mean_sb, op=ALU.subtract),
                    lambda t_=t_: dve.tensor_tensor(out=t_, in0=t_, in1=rstd_sb, op=ALU.mult),
                ], reads=[B_z[m], B_stat], writes=[bt_])
                gi = lidx * 8 + m

                def aff(m=m, t_=t_, gi=gi):
                    ins = act.activation(out=zt[:, m, :], in_=t_, func=AF.Identity, scale=lng[:, gi:gi + 1],
                                         bias=lnb[:, gi:gi + 1])
                    if want_bf16:
                        ins = act.activation(out=x1b[:, m, :], in_=t_, func=AF.Identity, scale=lng[:, gi:gi + 1],
                                             bias=lnb[:, gi:gi + 1])
                    return ins
                S.op("act", aff, reads=[bt_, B_small], writes=[B_z[m]] + ([B_x1b] if want_bf16 else []))
                yield

        def post_phase(l, last):
            la = {0: 0, 3: 1}.get(l)
            for i in range(8, 11):
                src = wgu_d[l, i * 4:(i + 1) * 4].rearrange("t p n -> p t n")
                S.dma("pool", lambda i=i, src=src: pool.dma_start(out=Wgu[:, i * 4:(i + 1) * 4, :], in_=src),
                      writes=[B_wgu[i]])
            Xsrc = xT if l == 0 else Xs
            Xdst = outT if last else Xs
            Xsv = Xsrc.rearrange("(c p) t -> p c t", p=128)
            Xdv = Xdst.rearrange("(c p) t -> p c t", p=128)
            OTv = OT.rearrange("(c p) t -> p c t", p=128)
            NCH = SEQ // TP

            otc_done = set()

            def load_otc(i):
                if i in otc_done or i >= NCH:
                    return
                otc_done.add(i)
                tsl_ = slice(i * TP, (i + 1) * TP)
                S.dma("sp", lambda: sp.dma_start(out=oTc, in_=OTv[:, :, tsl_]), writes=[B_oTc])

            def gen_a(i):
                zt, B_z = ZT[i % 2], BZ[i % 2]
                tsl = slice(i * TP, (i + 1) * TP)
                load_otc(i)
                for m in range(8):
                    S.dma("sp", lambda m=m: sp.dma_start(out=zt[:, m, :], in_=Xsv[:, m, tsl]), writes=[B_z[m]])
                yield
                for m in range(8):
                    wt, bwt = wos[ctr["wo"] % 2]
                    ctr["wo"] += 1
                    S.dma("pool", lambda m=m, wt=wt: pool.dma_start(out=wt, in_=wo_d[l, m]), writes=[bwt])
                    b = 2 + ctr["ob"] % 2
                    ctr["ob"] += 1
                    wv = wt.rearrange("p (c j) -> p c j", c=8)

                    def mmo(b=b, wv=wv):
                        for c in range(8):
                            ins = pe.matmul(PS[b][:, :], lhsT=wv[:, c, :], rhs=oTc[:, c, :], start=(c == 0),
                                            stop=(c == 7))
                        return ins
                    S.op("pe", mmo, reads=[bwt, B_oTc], writes=[PB[b]])
                    rf_ = [lambda b=b, m=m: dve.scalar_tensor_tensor(out=zt[:, m, :], in0=zt[:, m, :], scalar=ALPHA,
                                                                     in1=PS[b][:, :], op0=ALU.mult, op1=ALU.add)]
                    if la is not None:
                        rf_.append(lambda m=m: dve.tensor_scalar(out=zt[:, m, :], in0=zt[:, m, :],
                                                                 scalar1=bo[:, la * 8 + m:la * 8 + m + 1],
                                                                 scalar2=None, op0=ALU.add))
                    S.chain("dve", rf_, reads=[PB[b], B_small], writes=[B_z[m]])
                    yield
                yield from layer_norm(l * 2 + 0, zt, B_z, True)

            wd_state = {"issued": {}}

            def wd_issue(i, m):
                if (i, m) in wd_state["issued"] or i >= NCH or m >= 8:
                    return
                wt, bwt = wds[ctr["wd"] % 4]
                ctr["wd"] += 1
                wd_state["issued"][(i, m)] = (wt, bwt)
                for h_ in range(2):
                    S.dma("pool", lambda h_=h_, wt=wt: pool.dma_start(
                        out=wt[:, h_ * 11:(h_ + 1) * 11, :].rearrange("p f j -> p (f j)"),
                        in_=wd_d[l, m, :, h_ * 1408:(h_ + 1) * 1408]), writes=[bwt])

            def gen_b(i):
                zt, B_z = ZT[i % 2], BZ[i % 2]
                for m in range(4):
                    wd_issue(i, m)
                for m in range(8):
                    wt, bwt = wd_state["issued"][(i, m)]
                    b = 2 + ctr["ob"] % 2
                    ctr["ob"] += 1

                    def mmd(b=b, wt=wt):
                        for f in range(NF):
                            ins = pe.matmul(PS[b][:, :], lhsT=wt[:, f, :], rhs=aT[:, f, :], start=(f == 0),
                                            stop=(f == NF - 1))
                        return ins
                    S.op("pe", mmd, reads=[bwt] + B_aT, writes=[PB[b]])
                    S.op("dve", lambda b=b, m=m: dve.scalar_tensor_tensor(
                        out=zt[:, m, :], in0=zt[:, m, :], scalar=ALPHA, in1=PS[b][:, :], op0=ALU.mult, op1=ALU.add),
                        reads=[PB[b]], writes=[B_z[m]])
                    wd_issue(i, m + 4)
                    yield

            def gen_c(i):
                for f in range(NF):
                    k = ctr["gu"] % 2
                    ctr["gu"] += 1
                    gbk = 4 + k
                    ubk = 6 + k

                    def mmg(f=f, gbk=gbk, ubk=ubk):
                        wg = Wgu[:, f, :].rearrange("p (c j) -> p c j", c=8)
                        wu = Wgu[:, NF + f, :].rearrange("p (c j) -> p c j", c=8)
                        for c in range(8):
                            pe.matmul(PS[gbk][:, :], lhsT=wg[:, c, :], rhs=x1b[:, c, :], start=(c == 0), stop=(c == 7))
                        for c in range(8):
                            ins = pe.matmul(PS[ubk][:, :], lhsT=wu[:, c, :], rhs=x1b[:, c, :], start=(c == 0),
                                            stop=(c == 7))
                        return ins
                    S.op("pe", mmg, reads=[B_wgu[f // 4], B_wgu[(NF + f) // 4], B_x1b], writes=[PB[gbk], PB[ubk]])
                    sgt, bsg = sg[ctr["sg"] % 2]
                    ctr["sg"] += 1
                    S.op("act", lambda gbk=gbk, sgt=sgt: act.activation(out=sgt, in_=PS[gbk][:, :], func=AF.Silu),
                         reads=[PB[gbk]], writes=[bsg])
                    S.op("dve", lambda f=f, ubk=ubk, sgt=sgt: dve.tensor_tensor(out=aT[:, f, :], in0=sgt,
                                                                                 in1=PS[ubk][:, :], op=ALU.mult),
                         reads=[PB[ubk], bsg], writes=[B_aT[f]])
                    if f in (8, 11, 14, 17):
                        wd_issue(i, (f - 8) // 3)
                    yield

            def gen_d(i):
                zt, B_z = ZT[i % 2], BZ[i % 2]
                tsl = slice(i * TP, (i + 1) * TP)
                yield from layer_norm(l * 2 + 1, zt, B_z, False)
                for m in range(8):
                    S.dma("sp", lambda m=m: sp.dma_start(out=Xdv[:, m, tsl], in_=zt[:, m, :]), reads=[B_z[m]])
                yield

            run_all(gen_a(0))
            run_all(gen_c(0))
            for i in range(NCH):
                if i + 1 < NCH:
                    ga = gen_a(i + 1)
                    for _ in range(9):
                        next(ga)
                    interleave(ga, gen_b(i))
                    load_otc(i + 2)
                    interleave(gen_c(i + 1), gen_d(i))
                else:
                    run_all(gen_b(i))
                    run_all(gen_d(i))

        for l in range(n_layers):
            kind = KINDS[l]
            la = {0: 0, 3: 1}.get(l)
            att_common_start(l)
            if kind == 1:
                groups = [(1, 0), (4, 28), (16, 56)]
                mask = maskB
            else:
                groups = [(1, 0)]
                mask = maskA
            rope_prepass(l, groups, la)
            S.barrier()
            att_init_kv()
            if kind == 2:
                S.op("pool", lambda: pool.memset(SEL, 0.0), writes=[B_sel])
                units = [(p, 0, 1, 0) for p in range(8)]
            else:
                units = [(p, g, d, tb_) for p in range(8) for g, (d, tb_) in enumerate(groups)]
            lac = None if kind == 2 else la
            run_all(project_pair(l, units[0][0], units[0][1], units[0][2], units[0][3], lac, 0))
            for ui, (p, g, d, tb_) in enumerate(units):
                bs = ui % 2
                if kind == 2:
                    att = moba_pair(p, bs)
                else:
                    att = banded_attention(d, mask, g == 0, bs)
                nxt = None
                if ui + 1 < len(units):
                    p2, g2, d2, tb2 = units[ui + 1]
                    nxt = project_pair(l, p2, g2, d2, tb2, lac, 1 - bs)
                if nxt is None:
                    for i in range(8):
                        src = wgu_d[l, i * 4:(i + 1) * 4].rearrange("t p n -> p t n")
                        S.dma("pool", lambda i=i, src=src: pool.dma_start(out=Wgu[:, i * 4:(i + 1) * 4, :], in_=src),
                              writes=[B_wgu[i], B_xb])
                interleave(att, nxt)
                if kind != 2 and g == len(groups) - 1:
                    normalize_store(p, la)
            S.barrier()
            if stop_after_att and l == n_layers - 1:
                break
            post_phase(l, last=(l == n_layers - 1))
            S.barrier()
        S.barrier()
        S.emit(blk)
    return nc


def _tile(W, cols):
    t = W[:, cols].reshape(8, 128, 128).transpose(1, 0, 2)
    return np.ascontiguousarray(t).reshape(128, 1024)


def _att_tiles(kind, W):
    tiles = []
    hd = np.arange(64)
    if kind == 0:
        qcol = lambda h: h * 64 + hd
        kcol = lambda h: 1024 + (h // 4) * 64 + hd
        vcol = lambda h: 1280 + (h // 4) * 64 + hd
        groups = [(qcol, kcol, vcol)]
    elif kind == 1:
        groups = []
        for g in range(3):
            groups.append((lambda h, g=g: ((g * 3 + 0) * 16 + h) * 64 + hd,
                           lambda h, g=g: ((g * 3 + 1) * 16 + h) * 64 + hd,
                           lambda h, g=g: ((g * 3 + 2) * 16 + h) * 64 + hd))
    else:
        groups = [(lambda h: (0 * 16 + h) * 64 + hd, lambda h: (1 * 16 + h) * 64 + hd,
                   lambda h: (2 * 16 + h) * 64 + hd)]
    for (qc, kc, vc) in groups:
        for fc in (qc, kc):
            for half in range(2):
                cols = np.concatenate([fc(h)[half * 8:(half + 1) * 8] for h in range(16)])
                tiles.append(cols)
        for p in range(8):
            for fc in (qc, kc, vc):
                tiles.append(np.concatenate([fc(2 * p), fc(2 * p + 1)]))
    return tiles


def _prep_common(inp):
    f32 = np.float32
    com = {}
    com["invf"] = np.tile((500000.0 ** (-np.arange(0, 16, 2, dtype=np.float32) / 16)).astype(f32), 16).reshape(128, 1)
    lg = np.asarray(inp["ln_g"], f32).reshape(8, 8, 128)
    lb = np.asarray(inp["ln_b"], f32).reshape(8, 8, 128)
    com["lng"] = np.ascontiguousarray(lg.transpose(2, 0, 1).reshape(128, 64))
    com["lnb"] = np.ascontiguousarray(lb.transpose(2, 0, 1).reshape(128, 64))
    srcs = [(0, inp["a_w_qkv"][0]), (1, inp["b_w_qkv"][0]), (2, inp["c_w_qkv"][0]), (0, inp["a_w_qkv"][1])]
    batt = np.zeros((128, 56), f32)
    bv = np.zeros((2, 8, 128), f32)
    for l, (kind, W) in enumerate(srcs):
        W = np.asarray(W, f32)
        cols = _att_tiles(kind, W)
        com[f"watt{l}"] = np.stack([_tile(W, c) for c in cols])
        if kind == 0:
            la = 0 if l == 0 else 1
            bq = np.asarray(inp["a_b_qkv"][la], f32)
            for t, c in enumerate(cols):
                batt[:, la * 28 + t] = bq[c]
            for p in range(8):
                bv[la, p] = bq[cols[4 + 3 * p + 2]]
    com["batt"] = batt
    com["bv"] = bv
    wos = [inp["a_w_o"][0], inp["b_w_o"][0], inp["c_w_o"][0], inp["a_w_o"][1]]
    com["wo"] = np.stack([np.stack([_tile(np.asarray(w, f32), np.arange(m * 128, (m + 1) * 128)) for m in range(8)])
                          for w in wos])
    wgu = np.asarray(inp["w_gate_up"], f32)
    com["wgu"] = np.stack([np.stack([_tile(wgu[l], np.arange(f * 128, (f + 1) * 128)) for f in range(44)])
                           for l in range(4)])
    wd = np.asarray(inp["w_down"], f32)
    com["wd"] = np.ascontiguousarray(
        wd.reshape(4, NF, 128, 8, 128).transpose(0, 3, 2, 1, 4).reshape(4, 8, 128, NF * 128))
    bo = np.asarray(inp["a_b_o"], f32).reshape(2, 8, 128)
    com["bo"] = np.ascontiguousarray(bo.transpose(2, 0, 1).reshape(128, 16))
    com["sinks"] = np.asarray(inp["a_sinks"], f32).reshape(1, 32)
    return com


_NC_CACHE = {}


def kernel(**inputs):
    x = np.asarray(inputs["x"], np.float32)
    pos = np.asarray(inputs["positions"], np.int32)
    com = _prep_common(inputs)
    if "nc" not in _NC_CACHE:
        _NC_CACHE["nc"] = build()
    nc = _NC_CACHE["nc"]
    in_maps = []
    for b in range(8):
        m = dict(com)
        m["xT"] = np.ascontiguousarray(x[b].T)
        m["pos"] = np.ascontiguousarray(pos[b].reshape(1, SEQ))
        in_maps.append(m)
    res = run_bass_kernel_spmd(nc, in_maps, core_ids=list(range(8)))
    out = np.stack([np.ascontiguousarray(res.results[b]["outT"].T) for b in range(8)])
    return out.astype(np.float32)
```
